# Optimizing a Trainium2 kernel written in Bass

```python
import jax, jax.numpy as jnp
from jax import lax
import numpy as np

D_MODEL = 1024
BATCH = 32
SEQ = 2048
DEPTH = 2
DEC_BATCH = 32
DEC_SEQ = 32
PAST_LEN = 1024

CHUNK = 64
N_MIXERS = 2
N_CONV_LAYERS = (DEPTH + 1) // 2
N_ATTN_LAYERS = DEPTH // 2
CONV_WIDTH = 3
N_HEADS = 16
HEAD_DIM = D_MODEL // N_HEADS
PAST_CHUNKS = 8
BAND_PAST = PAST_CHUNKS * CHUNK
BAND = BAND_PAST + CHUNK
MAX_REL_DIST = 128
D_FF = ((8 * D_MODEL + 3 * 256 - 1) // (3 * 256)) * 256
NORM_EPS = 1e-6
NEG_INF = -1e30

kernel_name = 'hybrid_shortconv_chunkattn_stream_step'


def rmsnorm(x, g):
    xf = x.astype(jnp.float32)
    inv = lax.rsqrt(jnp.mean(xf * xf, axis=-1, keepdims=True) + NORM_EPS)
    return (xf * inv * g.astype(jnp.float32)).astype(x.dtype)


def swiglu_ffn(h, w_gate_up, w_down):
    g, u = jnp.split(h @ w_gate_up, 2, axis=-1)
    return (jax.nn.silu(g) * u) @ w_down


def short_conv_mixer(h, conv_buf, w_in, conv_w, w_out):
    b_gate, c_gate, xh = jnp.split(h @ w_in, 3, axis=-1)
    u = c_gate * xh
    u_ext = jnp.concatenate([conv_buf, u], axis=1)
    T = u.shape[1]
    y = conv_w[0] * u_ext[:, 0:T]
    for tap in range(1, CONV_WIDTH):
        y = y + conv_w[tap] * u_ext[:, tap:tap + T]
    out = (b_gate * y) @ w_out
    return out, u_ext[:, -(CONV_WIDTH - 1):]


def _qkv(h, w_qkv):
    B, T, _ = h.shape
    q, k, v = jnp.split(h @ w_qkv, 3, axis=-1)
    shp = (B, T, N_HEADS, HEAD_DIM)
    return q.reshape(shp), k.reshape(shp), v.reshape(shp)


def _rel_bias(table, q_pos, k_pos):
    rel = jnp.clip(q_pos[:, None] - k_pos[None, :], -MAX_REL_DIST, MAX_REL_DIST) + MAX_REL_DIST
    return table[:, rel].astype(jnp.float32)


def _attend(q, k, v, bias, valid):
    s = jnp.einsum('bqhd,bkhd->bhqk', q, k).astype(jnp.float32) * (HEAD_DIM ** -0.5) + bias
    s = jnp.where(valid, s, NEG_INF)
    p = jax.nn.softmax(s, axis=-1)
    return jnp.einsum('bhqk,bkhd->bqhd', p.astype(v.dtype), v)


def chunk_attention_prompt(h, w_qkv, w_o, table):
    B, S, _ = h.shape
    nc = S // CHUNK
    q, k, v = _qkv(h, w_qkv)
    q_chunks = q.reshape(B, nc, CHUNK, N_HEADS, HEAD_DIM)
    pad = ((0, 0), (BAND_PAST, 0), (0, 0), (0, 0))
    k_pad = jnp.pad(k, pad)
    v_pad = jnp.pad(v, pad)
    band_pos = jnp.arange(BAND)
    bias = _rel_bias(table, jnp.arange(CHUNK) + BAND_PAST, band_pos)

    def one_chunk(c):
        q_c = lax.dynamic_index_in_dim(q_chunks, c, axis=1, keepdims=False)
        k_c = lax.dynamic_slice_in_dim(k_pad, c * CHUNK, BAND, axis=1)
        v_c = lax.dynamic_slice_in_dim(v_pad, c * CHUNK, BAND, axis=1)
        valid = (c * CHUNK - BAND_PAST + band_pos) >= 0
        return _attend(q_c, k_c, v_c, bias, valid)

    o = lax.map(one_chunk, jnp.arange(nc))
    o = jnp.moveaxis(o, 0, 1).reshape(B, S, D_MODEL)
    rows = min(BAND_PAST, S)
    return o @ w_o, k[:, S - rows:], v[:, S - rows:]


def chunk_attention_sample(h, cache_k, cache_v, w_qkv, w_o, table):
    B, T, _ = h.shape
    R = cache_k.shape[1]
    q, k, v = _qkv(h, w_qkv)
    k_all = jnp.concatenate([cache_k, k], axis=1)
    v_all = jnp.concatenate([cache_v, v], axis=1)
    bias = _rel_bias(table, R + jnp.arange(T), jnp.arange(R + T))
    o = _attend(q, k_all, v_all, bias, True).reshape(B, T, D_MODEL)
    return o @ w_o, k, v


def setup_inputs(seed: int = 0) -> dict:
    key = jax.random.key(seed)
    ks = jax.random.split(key, 20)
    f32 = jnp.float32

    def nrm(k, shape, scale):
        return jax.random.normal(k, shape, f32) * scale

    kv_rows = min(BAND_PAST, PAST_LEN)
    d_inv = D_MODEL ** -0.5
    return {
        'x_prompt': nrm(ks[0], (BATCH, SEQ, D_MODEL), 1.0),
        'x_sample': nrm(ks[1], (DEC_BATCH, DEC_SEQ, D_MODEL), 1.0),
        'state_conv': nrm(ks[2], (N_CONV_LAYERS, DEC_BATCH, CONV_WIDTH - 1, D_MODEL), 1.0),
        'cache_k': nrm(ks[3], (N_ATTN_LAYERS, DEC_BATCH, kv_rows, N_HEADS, HEAD_DIM), 1.0),
        'cache_v': nrm(ks[4], (N_ATTN_LAYERS, DEC_BATCH, kv_rows, N_HEADS, HEAD_DIM), 1.0),
        'conv_w_in': nrm(ks[5], (N_CONV_LAYERS, D_MODEL, 3 * D_MODEL), d_inv),
        'conv_kernel': nrm(ks[6], (N_CONV_LAYERS, CONV_WIDTH, D_MODEL), CONV_WIDTH ** -0.5),
        'conv_w_out': nrm(ks[7], (N_CONV_LAYERS, D_MODEL, D_MODEL), d_inv),
        'attn_w_qkv': nrm(ks[8], (N_ATTN_LAYERS, D_MODEL, 3 * D_MODEL), d_inv),
        'attn_w_o': nrm(ks[9], (N_ATTN_LAYERS, D_MODEL, D_MODEL), d_inv),
        'attn_rel_bias': nrm(ks[10], (N_ATTN_LAYERS, N_HEADS, 2 * MAX_REL_DIST + 1), 0.5),
        'norm_mix_pre': 1.0 + nrm(ks[11], (DEPTH, D_MODEL), 0.05),
        'norm_mix_post': 1.0 + nrm(ks[12], (DEPTH, D_MODEL), 0.05),
        'norm_ffn_pre': 1.0 + nrm(ks[13], (DEPTH, D_MODEL), 0.05),
        'norm_ffn_post': 1.0 + nrm(ks[14], (DEPTH, D_MODEL), 0.05),
        'ffn_w_gate_up': nrm(ks[15], (DEPTH, D_MODEL, 2 * D_FF), d_inv),
        'ffn_w_down': nrm(ks[16], (DEPTH, D_FF, D_MODEL), D_FF ** -0.5),
    }


def reference(x_prompt, x_sample, state_conv, cache_k, cache_v, conv_w_in, conv_kernel, conv_w_out,
              attn_w_qkv, attn_w_o, attn_rel_bias, norm_mix_pre, norm_mix_post, norm_ffn_pre,
              norm_ffn_post, ffn_w_gate_up, ffn_w_down):
    xp, xs = x_prompt, x_sample
    conv_p, conv_s, kp_l, vp_l, ks_l, vs_l = [], [], [], [], [], []
    for i in range(DEPTH):
        j = i // N_MIXERS
        hp = rmsnorm(xp, norm_mix_pre[i])
        hs = rmsnorm(xs, norm_mix_pre[i])
        if i % N_MIXERS == 0:
            zero_buf = jnp.zeros((xp.shape[0], CONV_WIDTH - 1, D_MODEL), xp.dtype)
            mp, bp = short_conv_mixer(hp, zero_buf, conv_w_in[j], conv_kernel[j], conv_w_out[j])
            ms, bs = short_conv_mixer(hs, state_conv[j], conv_w_in[j], conv_kernel[j], conv_w_out[j])
            conv_p.append(bp)
            conv_s.append(bs)
        else:
            mp, kp, vp = chunk_attention_prompt(hp, attn_w_qkv[j], attn_w_o[j], attn_rel_bias[j])
            ms, kn, vn = chunk_attention_sample(hs, cache_k[j], cache_v[j], attn_w_qkv[j], attn_w_o[j],
                                                attn_rel_bias[j])
            kp_l.append(kp)
            vp_l.append(vp)
            ks_l.append(kn)
            vs_l.append(vn)
        xp = xp + rmsnorm(mp, norm_mix_post[i])
        xs = xs + rmsnorm(ms, norm_mix_post[i])
        fp = swiglu_ffn(rmsnorm(xp, norm_ffn_pre[i]), ffn_w_gate_up[i], ffn_w_down[i])
        fs = swiglu_ffn(rmsnorm(xs, norm_ffn_pre[i]), ffn_w_gate_up[i], ffn_w_down[i])
        xp = xp + rmsnorm(fp, norm_ffn_post[i])
        xs = xs + rmsnorm(fs, norm_ffn_post[i])
    return (xp, xs, jnp.stack(conv_p), jnp.stack(conv_s), jnp.stack(kp_l), jnp.stack(vp_l),
            jnp.stack(ks_l), jnp.stack(vs_l))
```

```python
import contextlib
import numpy as np
import concourse.bass as bass
import concourse.mybir as mybir
from concourse.bass_utils import run_bass_kernel_spmd

F32 = mybir.dt.float32
BF16 = mybir.dt.bfloat16
AF = mybir.ActivationFunctionType
ALU = mybir.AluOpType

NCORES = 8
D = 1024
DFF = 2816
NFC = 22
SEQ = 2048
TWP = 512
NSLOT = 6
SLOTW = 2816
EPOCH = 12000
SAME_ENGINE_SYNC = True
EPS = 1e-6
ADD_ENG = "dve"


class Op:
    __slots__ = ("eng", "fn", "deps", "is_dma", "slot", "sem", "val", "signals", "idx", "ndma", "seq")


class Prog:
    def __init__(self):
        self.ops = []
        self.lastw = {}
        self.readers = {}
        self.dma_count = {}
        self.bank_rr = {}

    @staticmethod
    def _k(o):
        return ("dma", o.slot) if o.is_dma else ("eng", o.eng)

    def add(self, eng, fn, reads=(), writes=(), slot=None, ndma=1):
        op = Op()
        op.eng = eng
        op.fn = fn
        op.is_dma = slot is not None
        op.slot = slot
        op.ndma = ndma
        op.idx = len(self.ops)
        op.signals = False
        op.sem = None
        op.val = None
        deps = {}
        op_key_holder = [op]

        def dep(o):
            if o is None or o is op:
                return
            k = self._k(o)
            if k not in deps or deps[k].idx < o.idx:
                deps[k] = o

        for k in reads:
            dep(self.lastw.get(k))
            if isinstance(k, tuple) and k[0] == "ps":
                for rk, r in self.readers.get(k, {}).items():
                    if rk != self._k(op_key_holder[0]):
                        dep(r)
        for k in writes:
            dep(self.lastw.get(k))
            for r in self.readers.get(k, {}).values():
                dep(r)
        for k in writes:
            self.lastw[k] = op
            self.readers[k] = {}
        for k in reads:
            self.readers.setdefault(k, {})[self._k(op)] = op
        if op.is_dma:
            c = self.dma_count.get(slot, 0) + ndma
            self.dma_count[slot] = c
            op.val = 16 * c
        op.deps = []
        for k, o in deps.items():
            if k == ("eng", eng) and (eng == "pe" or not SAME_ENGINE_SYNC):
                continue
            op.deps.append(o)
            o.signals = True
        self.ops.append(op)
        return op

    def finalize(self, nc, stack):
        per = {}
        for op in self.ops:
            per.setdefault(op.eng, []).append(op)
        self.eng_sems = {}
        for eng, ops in per.items():
            n = 0
            for op in ops:
                if op.is_dma or not op.signals:
                    continue
                op.seq = n
                n += 1
            nep = (n + EPOCH - 1) // EPOCH
            sems = [stack.enter_context(nc.semaphore(f"s_{eng}_{i}")) for i in range(max(nep, 1))]
            self.eng_sems[eng] = sems
            for op in ops:
                if op.is_dma or not op.signals:
                    continue
                op.sem = sems[op.seq // EPOCH]
                op.val = op.seq % EPOCH + 1
        self.dma_sems = {}
        for slot in self.dma_count:
            self.dma_sems[slot] = stack.enter_context(nc.semaphore(f"d_{slot}"))
        for op in self.ops:
            if op.is_dma:
                op.sem = self.dma_sems[op.slot]
        return per

    def emit_engine(self, eng_name, eng, ops, final_wait=False):
        known = {}
        for op in ops:
            for d in op.deps:
                key = id(d.sem)
                if known.get(key, 0) >= d.val:
                    continue
                eng.wait_ge(d.sem, d.val)
                known[key] = d.val
            r = op.fn(eng)
            if op.is_dma:
                insts = r if isinstance(r, (list, tuple)) else [r]
                assert len(insts) == op.ndma, (len(insts), op.ndma)
                for ins in insts:
                    ins.then_inc(op.sem, 16)
            else:
                insts = r if isinstance(r, (list, tuple)) else [r]
                if op.signals:
                    insts[-1].then_inc(op.sem, 1)
        if final_wait:
            for slot, sem in self.dma_sems.items():
                eng.wait_ge(sem, 16 * self.dma_count[slot])

    def bank(self, pool="mm", banks=(0, 1, 2, 3, 4, 5, 6, 7)):
        i = self.bank_rr.get(pool, 0)
        self.bank_rr[pool] = i + 1
        return banks[i % len(banks)]


class _Stop(Exception):
    pass


def build_program(nc, stack, n_prompt_tiles=16, do_sample=True, do_l1=True, stop=None, skip=()):
    P = Prog()

    def checkpoint(name):
        if stop == name:
            raise _Stop()
    try:
        _build_body(nc, stack, P, n_prompt_tiles, do_sample, do_l1, checkpoint, skip)
    except _Stop:
        pass
    return P


def _build_body(nc, stack, P, n_prompt_tiles, do_sample, do_l1, checkpoint, skip):

    def din(name, shape):
        return nc.dram_tensor(name, list(shape), F32, kind="ExternalInput")

    def dout(name, shape):
        return nc.dram_tensor(name, list(shape), F32, kind="ExternalOutput")

    xp_d = din("xp", [4 * SEQ, D])
    xs_d = din("xs", [128, D])
    sconv_d = din("sconv", [8, D])
    ck_d = din("ck", [4 * 512, D])
    cv_d = din("cv", [4 * 512, D])
    w_d = {
        "win0": din("win0", [D, 3 * D]), "wout0": din("wout0", [D, D]),
        "wqkv": din("wqkv", [D, 3 * D]), "wo": din("wo", [D, D]),
        "wgu0": din("wgu0", [D, 2 * DFF]), "wgu1": din("wgu1", [D, 2 * DFF]),
        "wdn0": din("wdn0", [DFF, D]), "wdn1": din("wdn1", [DFF, D]),
    }
    convk_d = din("convk", [3, D])
    relb_d = din("relb", [16, 257])
    norms_d = din("norms", [8, D])
    ident_d = din("ident", [128, 128])

    yp_d = dout("yp", [4 * SEQ, D])
    ys_d = dout("ys", [128, D])
    ncp_d = dout("ncp", [8, D])
    ncs_d = dout("ncs", [8, D])
    kp_d = dout("kp", [4 * 512, D])
    vp_d = dout("vp", [4 * 512, D])
    ks_d = dout("ks", [128, D])
    vs_d = dout("vs", [128, D])

    wscr = {}
    for name, t in w_d.items():
        if name.startswith("wdn"):
            wscr[name] = nc.dram_tensor("s_" + name, [8, 128, NFC * 128], BF16, kind="Internal")
        else:
            C = t.shape[1]
            wscr[name] = nc.dram_tensor("s_" + name, [C // 256, 128, 8 * 256], BF16, kind="Internal")
    r2_d = nc.dram_tensor("s_r2", [128, 16 * 383], F32, kind="Internal")

    def sb(name, shape, dt):
        return stack.enter_context(nc.sbuf_tensor("sbt_" + name, list(shape), dt))

    x = sb("x", [128, 8, 512], F32)
    xin = sb("xin", [128, 4, 1024], F32)
    h = sb("h", [128, 8, 512], BF16)
    sqv = sb("sqv", [128, 8, 512], BF16)
    sq = sb("sq", [128, 8, 512], BF16)
    m = sb("m", [128, 8, 512], F32)
    rs = sb("rs", [128, 512], F32)
    rstd = sb("rstd", [128, 512], F32)
    kT = sb("kT", [128, 8, 1024], BF16)
    Vb = sb("Vb", [128, 8, 8, 192], BF16)
    wslot = sb("wslot", [128, NSLOT, SLOTW], BF16)
    BT = sb("BT", [128, 16, 256], BF16)
    ident = sb("ident_sb", [128, 128], F32)
    ident_bf = sb("ident_bf", [128, 128], BF16)
    ones_bf = sb("ones_bf", [128, 128], BF16)
    gT = sb("gT", [128, 8, 8], F32)
    cwT = sb("cwT", [128, 8, 3], F32)
    carry = sb("carry", [128, 8, 2], F32)
    small_in = sb("small_in", [8, 1024], F32)
    ncst = sb("ncst", [8, 1024], F32)
    small_in2 = ncst
    eps_t = sb("eps_t", [128, 1], F32)
    chb = sb("chb", [128, 16], F32)
    dummy = sb("dummy_t", [128, 2], F32)
    ARENA = 10240
    arena = sb("arena", [128, ARENA], F32)

    psb = [stack.enter_context(nc.psum_tensor(f"ps{i}", [128, 512], F32)) for i in range(8)]

    def a_f32(off, n):
        return arena[:, off:off + n]

    def a_bf16(off, n_bf):
        return arena[:, off:off + n_bf // 2].bitcast(BF16)

    u_ext = a_f32(0, 8 * 514).rearrange("p (j t) -> p j t", j=8)
    u_ext_s = a_f32(0, 8 * 136).rearrange("p (j s t) -> p j s t", j=8, s=4)
    xh_t = [a_f32(4112 + i * 512, 512) for i in range(2)]
    y_t = [a_f32(4112 + 1024 + i * 512, 512) for i in range(2)]
    ul_s = a_f32(4112 + 2048, 64).rearrange("p (j s r) -> p j s r", j=8, s=4)
    a_act = a_bf16(0, NFC * 512).rearrange("p (j t) -> p j t", j=NFC)
    sg_t = [a_f32(5632 + i * 512, 512) for i in range(2)]
    qT = a_bf16(0, 8 * 512).rearrange("p (j t) -> p j t", j=8)
    qTz = a_bf16(0, 8 * 2 * 512).rearrange("p (j e t) -> p j e t", j=8, e=2)
    Vn = a_bf16(2048, 4 * 1024).rearrange("p (b c) -> p b c", b=4)
    PT = [a_bf16(4096 + i * 256, 512) for i in range(4)]
    rec_t = [a_f32(5120 + i * 512, 512) for i in range(2)]
    kTc = a_bf16(6144, 8 * 512).rearrange("p (j t) -> p j t", j=8)
    Vc = a_bf16(8192, 4 * 1024).rearrange("p (b c) -> p b c", b=4)
    tabs = a_f32(0, 16 * 257).rearrange("p (h e) -> p h e", h=16)
    E2 = a_f32(4112, 16 * 383).rearrange("p (h e) -> p h e", h=16)

    phase_ctr = [0]

    def arena_switch():
        P.add("dve", lambda e: e.memset(dummy[:, 0:1], 0.0), reads=(), writes=("arena_phase",))

    AR = ("arena_phase",)

    P.add("sp", lambda e: e.dma_start(out=ident[:], in_=ident_d.ap()), writes=("ident",), slot="ld_ident")
    P.add("sp", lambda e: e.dma_start(out=small_in[:], in_=norms_d.ap()), writes=("small_in",), slot="ld_small")
    P.add("sp", lambda e: e.dma_start(out=small_in2[0:3, :], in_=convk_d.ap()), writes=(("ncst", 0), ("ncst", 1)), slot="ld_small2")
    P.add("sp", lambda e: e.dma_start(out=tabs, in_=bass.AP(relb_d, 0, [[0, 128], [257, 16], [1, 257]])),
          reads=AR, writes=("tabs",), slot="ld_tabs")
    P.add("dve", lambda e: e.memset(ones_bf[:], 1.0), writes=("ones",))
    P.add("dve", lambda e: e.memset(eps_t[:], EPS), writes=("eps",))
    P.add("dve", lambda e: e.tensor_copy(out=ident_bf[:], in_=ident[:]), reads=("ident",), writes=("ident_bf",))

    checkpoint("setup0")
    checkpoint("convw")

    b = P.bank()
    P.add("pe", lambda e, b=b: [e.transpose(out=psb[b][:, j * 8:(j + 1) * 8], in_=small_in[0:8, j * 128:(j + 1) * 128],
                                            identity=ident[0:8, 0:8]) for j in range(8)],
          reads=("small_in", "ident"), writes=(("ps", b),))
    P.add("dve", lambda e, b=b: e.tensor_copy(out=gT[:].rearrange("p j g -> p (j g)"), in_=psb[b][:, 0:64]),
          reads=(("ps", b),), writes=("gT",))
    b = P.bank()
    P.add("pe", lambda e, b=b: [e.transpose(out=psb[b][:, j * 3:(j + 1) * 3], in_=small_in2[0:3, j * 128:(j + 1) * 128],
                                            identity=ident[0:3, 0:3]) for j in range(8)],
          reads=(("ncst", 0), ("ncst", 1), "ident"), writes=(("ps", b),))
    P.add("dve", lambda e, b=b: e.tensor_copy(out=cwT[:].rearrange("p j g -> p (j g)"), in_=psb[b][:, 0:24]),
          reads=(("ps", b),), writes=("cwT",))

    checkpoint("gains")
    P.add("dve", lambda e: e.tensor_copy(out=chb[:].rearrange("p (h o) -> p h o", o=1), in_=tabs[:, :, 256:257]),
          reads=("tabs",) + AR, writes=("chb",))
    P.add("dve", lambda e: e.tensor_tensor(out=E2[:, :, 0:256], in0=tabs[:, :, 1:257],
                                           in1=chb[:].rearrange("p (h o) -> p h o", o=1).broadcast_to([128, 16, 256]),
                                           op=ALU.subtract),
          reads=("tabs", "chb") + AR, writes=("E2a",))
    P.add("dve", lambda e: e.memset(E2[:, :, 256:383], 0.0), reads=AR, writes=("E2b",))
    P.add("dve", lambda e: e.memset(Vb[:, :, :, 64:128], 1.0), reads=(), writes=("BTb",))
    P.add("pool", lambda e: e.dma_start(out=r2_d.ap(), in_=E2.rearrange("p h e -> p (h e)")),
          reads=("E2a", "E2b") + AR, writes=("r2",), slot="st_r2")
    P.add("pool", lambda e: e.dma_start(out=BT[:, :, 0:256],
                                        in_=bass.AP(r2_d, 127, [[16 * 383 - 1, 128], [383, 16], [1, 256]])),
          reads=("r2",), writes=("BTa",), slot="ld_bt")
    BTK = ("BTa", "BTb")
    checkpoint("bt")

    wstate = {"i": 0}

    converted = set()

    def wload(name, piece):
        s = wstate["i"] % NSLOT
        wstate["i"] += 1
        t = w_d[name]
        if name.startswith("wdn"):
            dst = wslot[:, s, 0:NFC * 128]
            src32 = bass.AP(t, piece * 128, [[D, 128], [128 * D, NFC], [1, 128]])
            dst3 = dst.rearrange("p (k c) -> p k c", k=NFC)
        else:
            C = t.shape[1]
            dst = wslot[:, s, 0:2048]
            src32 = bass.AP(t, piece * 256, [[C, 128], [128 * C, 8], [1, 256]])
            dst3 = dst.rearrange("p (k c) -> p k c", k=8)
        scr = wscr[name].ap()[piece]
        key = ("wscr", name, piece)
        if key not in converted:
            converted.add(key)
            P.add("pool", lambda e: e.dma_start(out=dst3, in_=src32), writes=(("wslot", s),), slot=f"wc{s}")
            P.add("sp", lambda e: e.dma_start(out=scr, in_=dst), reads=(("wslot", s),), writes=(key,), slot=f"wb{s}")
        else:
            P.add("sp", lambda e: e.dma_start(out=dst, in_=scr), reads=(key,), writes=(("wslot", s),), slot=f"w{s}")
        return s

    evac_rr = [0]

    def evac_copy(out_ap, in_ap, reads, writes, scale=None, eng=None):
        if eng is None:
            eng = "act" if evac_rr[0] % 2 == 0 else "dve"
            evac_rr[0] += 1
        if eng == "act":
            if scale is None:
                P.add("act", lambda e: e.activation(out=out_ap, in_=in_ap, func=AF.Copy), reads=reads, writes=writes)
            else:
                P.add("act", lambda e: e.activation(out=out_ap, in_=in_ap, func=AF.Copy, scale=scale),
                      reads=reads, writes=writes)
        else:
            if scale is None:
                P.add("dve", lambda e: e.tensor_copy(out=out_ap, in_=in_ap), reads=reads, writes=writes)
            else:
                P.add("dve", lambda e: e.tensor_scalar(out=out_ap, in0=in_ap, scalar1=scale, scalar2=None,
                                                       op0=ALU.mult), reads=reads, writes=writes)

    def norm_stats(TW):
        b = P.bank()
        for j in range(8):
            P.add("pe", lambda e, j=j: e.matmul(psb[b][:, 0:TW], lhsT=ones_bf[:], rhs=sq[:, j, 0:TW],
                                                start=(j == 0), stop=(j == 7)),
                  reads=[("sq", j), "ones"], writes=(("ps", b),))
        P.add("act", lambda e: e.activation(out=rs[:, 0:TW], in_=psb[b][:, 0:TW], func=AF.Ln,
                                            bias=eps_t[:, 0:1], scale=1.0 / D),
              reads=(("ps", b), "eps"), writes=("rs",))
        P.add("act", lambda e: e.activation(out=rstd[:, 0:TW], in_=rs[:, 0:TW], func=AF.Exp, scale=-0.5),
              reads=("rs",), writes=("rstd",))

    def prenorm(g, TW):
        for j in range(8):
            P.add("act", lambda e, j=j: e.activation(out=sq[:, j, 0:TW], in_=x[:, j, 0:TW], func=AF.Square),
                  reads=(("x", j),), writes=(("sq", j),))
        norm_stats(TW)
        for j in range(8):
            P.add("dve", lambda e, j=j: e.scalar_tensor_tensor(out=h[:, j, 0:TW], in0=x[:, j, 0:TW],
                                                               scalar=gT[:, j, g:g + 1], in1=rstd[:, 0:TW],
                                                               op0=ALU.mult, op1=ALU.mult),
                  reads=(("x", j), "rstd", "gT"), writes=(("h", j),))

    def postnorm(g, TW, produce):
        for n in range(8):
            b = produce(n)
            checkpoint("pn_mm")
            P.add("act", lambda e, b=b, n=n: e.activation(out=sq[:, n, 0:TW], in_=psb[b][:, 0:TW], func=AF.Square),
                  reads=(("ps", b),), writes=(("sq", n),))
            checkpoint("pn_sq")
            P.add("dve", lambda e, b=b, n=n: e.tensor_copy(out=m[:, n, 0:TW], in_=psb[b][:, 0:TW]),
                  reads=(("ps", b),), writes=(("m", n),))
            checkpoint("pn_a%d" % n)
        checkpoint("pn_a")
        norm_stats(TW)
        checkpoint("pn_b")
        def stt(n):
            P.add("dve", lambda e, n=n: e.scalar_tensor_tensor(out=m[:, n, 0:TW], in0=m[:, n, 0:TW],
                                                               scalar=gT[:, n, g:g + 1], in1=rstd[:, 0:TW],
                                                               op0=ALU.mult, op1=ALU.mult),
                  reads=(("m", n), "rstd", "gT"), writes=(("m", n),))

        def add(n):
            P.add(ADD_ENG, lambda e, n=n: e.tensor_tensor(out=x[:, n, 0:TW], in0=x[:, n, 0:TW], in1=m[:, n, 0:TW],
                                                         op=ALU.add),
                  reads=(("x", n), ("m", n)), writes=(("x", n),))
        stt(0)
        for n in range(8):
            if n + 1 < 8:
                stt(n + 1)
            add(n)

    mm_pool = [None]

    def mmbank():
        return P.bank(*mm_pool[0]) if mm_pool[0] else P.bank()

    def proj_fm(name, piece, half, s, rhs_fn, rhs_keys, TW, nk=8):
        b = mmbank()
        P.add("pe", lambda e: [e.matmul(psb[b][:, 0:TW], lhsT=wslot[:, s, k * 256 + half * 128:k * 256 + half * 128 + 128],
                                        rhs=rhs_fn(k), start=(k == 0), stop=(k == nk - 1)) for k in range(nk)],
              reads=list(rhs_keys) + [("wslot", s)], writes=(("ps", b),))
        return b

    def proj_h_multi(groups, TW):
        banks = [P.bank() for _ in groups]
        for k in range(8):
            P.add("pe", lambda e, k=k: [e.matmul(psb[b][:, 0:TW],
                                                 lhsT=wslot[:, s, k * 256 + half * 128:k * 256 + half * 128 + 128],
                                                 rhs=h[:, k, 0:TW], start=(k == 0), stop=(k == 7))
                                        for b, (s, half) in zip(banks, groups)],
                  reads=[("h", k)] + [("wslot", s) for s, _ in groups], writes=[("ps", b) for b in banks])
        return banks

    def out_proj(name, TW, kouter_banks=None):
        st = {}

        def produce(n):
            piece, half = n // 2, n % 2
            if kouter_banks is not None and n < 4:
                if n == 0:
                    sl = [wload(name, 0), wload(name, 1)]
                    groups = [(sl[0], 0), (sl[0], 1), (sl[1], 0), (sl[1], 1)]
                    st["kb"] = list(kouter_banks)
                    for k in range(8):
                        P.add("pe", lambda e, k=k, groups=groups: [
                            e.matmul(psb[b][:, 0:TW], lhsT=wslot[:, s, k * 256 + hf * 128:k * 256 + hf * 128 + 128],
                                     rhs=sqv[:, k, 0:TW], start=(k == 0), stop=(k == 7))
                            for b, (s, hf) in zip(st["kb"], groups)],
                            reads=[("sqv", k)] + [("wslot", s) for s, _ in groups],
                            writes=[("ps", b) for b in st["kb"]])
                return st["kb"][n]
            if half == 0:
                st["s"] = wload(name, piece)
            return proj_fm(name, piece, half, st["s"], lambda k: sqv[:, k, 0:TW], [("sqv", k) for k in range(8)], TW)
        return produce

    def conv_mixer(tl):
        TW, sample = tl["TW"], tl["sample"]
        arena_switch()
        if sample:
            P.add("sp", lambda e: e.dma_start(out=small_in[:], in_=sconv_d.ap()), writes=("small_in",), slot="ld_small")
            b = P.bank()
            P.add("pe", lambda e, b=b: [e.transpose(out=psb[b][:, j * 8:(j + 1) * 8],
                                                    in_=small_in[0:8, j * 128:(j + 1) * 128], identity=ident[0:8, 0:8])
                                        for j in range(8)],
                  reads=("small_in", "ident"), writes=(("ps", b),))
            P.add("dve", lambda e, b=b: e.tensor_copy(out=u_ext_s[:, :, :, 0:2],
                                                 in_=psb[b][:, 0:64].rearrange("p (j s r) -> p j s r", j=8, s=4)),
                  reads=(("ps", b),) + AR, writes=[("u", j) for j in range(8)])
        elif tl["first"]:
            P.add("dve", lambda e: e.memset(u_ext[:, :, 0:2], 0.0), reads=AR, writes=[("u", j) for j in range(8)])
        else:
            P.add("dve", lambda e: e.tensor_copy(out=u_ext[:, :, 0:2], in_=carry[:]),
                  reads=("carry",) + AR, writes=[("u", j) for j in range(8)])

        def v3(ap):
            return ap.rearrange("p (s t) -> p s t", s=4) if sample else ap

        def utap(jc, k):
            return u_ext_s[:, jc, :, k:k + 32] if sample else u_ext[:, jc, k:k + TW]

        for i in range(4):
            sl = [wload("win0", i), wload("win0", 4 + i), wload("win0", 8 + i)]
            pro = proj_h_multi([(sl[0], 0), (sl[1], 0), (sl[2], 0), (sl[0], 1), (sl[1], 1), (sl[2], 1)], TW) if i == 0 else None
            for half in range(2):
                jc = 2 * i + half
                hk = [("h", k) for k in range(8)]
                if pro is not None:
                    bb, bc, bx = pro[3 * half:3 * half + 3]
                else:
                    bb = proj_fm("win0", i, half, sl[0], lambda k: h[:, k, 0:TW], hk, TW)
                    bc = proj_fm("win0", 4 + i, half, sl[1], lambda k: h[:, k, 0:TW], hk, TW)
                    bx = proj_fm("win0", 8 + i, half, sl[2], lambda k: h[:, k, 0:TW], hk, TW)
                xt = xh_t[jc % 2]
                yt = y_t[jc % 2]
                P.add("act", lambda e, bx=bx, xt=xt: e.activation(out=xt[:, 0:TW], in_=psb[bx][:, 0:TW], func=AF.Copy),
                      reads=(("ps", bx),) + AR, writes=(("xh_t", jc % 2),))
                P.add("dve", lambda e, bc=bc, xt=xt, jc=jc: e.tensor_tensor(out=utap(jc, 2), in0=v3(psb[bc][:, 0:TW]),
                                                                           in1=v3(xt[:, 0:TW]), op=ALU.mult),
                      reads=(("ps", bc), ("xh_t", jc % 2)) + AR, writes=(("u", jc),))
                P.add("pool", lambda e, jc=jc, yt=yt: e.tensor_scalar(out=v3(yt[:, 0:TW]), in0=utap(jc, 0),
                                                                      scalar1=cwT[:, jc, 0:1], scalar2=0.0,
                                                                      op0=ALU.mult, op1=ALU.add),
                      reads=(("u", jc), "cwT") + AR, writes=(("y_t", jc % 2),))
                for k in (1, 2):
                    P.add("dve", lambda e, jc=jc, yt=yt, k=k: e.scalar_tensor_tensor(
                        out=v3(yt[:, 0:TW]), in0=utap(jc, k), scalar=cwT[:, jc, k:k + 1], in1=v3(yt[:, 0:TW]),
                        op0=ALU.mult, op1=ALU.add),
                        reads=(("u", jc), "cwT", ("y_t", jc % 2)) + AR, writes=(("y_t", jc % 2),))
                P.add("dve", lambda e, bb=bb, jc=jc, yt=yt: e.tensor_tensor(out=sqv[:, jc, 0:TW], in0=psb[bb][:, 0:TW],
                                                                           in1=yt[:, 0:TW], op=ALU.mult),
                      reads=(("ps", bb), ("y_t", jc % 2)) + AR, writes=(("sqv", jc),))
        uk = [("u", j) for j in range(8)]
        if sample:
            P.add("dve", lambda e: e.tensor_copy(out=ul_s, in_=u_ext_s[:, :, :, 32:34]), reads=uk + list(AR),
                  writes=("ul_s",))
            R = 8
            src_fn = lambda jc: ul_s[:, jc, :, :].rearrange("p s r -> p (s r)")
            skeys = ["ul_s"] + list(AR)
        else:
            P.add("dve", lambda e: e.tensor_copy(out=carry[:], in_=u_ext[:, :, TW:TW + 2]), reads=uk + list(AR),
                  writes=("carry",))
            R = 2
            src_fn = lambda jc: carry[:, jc, :]
            skeys = ["carry"]
        if sample or tl["last"]:
            for hf in range(2):
                b = P.bank()
                P.add("pe", lambda e, b=b, hf=hf: [e.transpose(out=psb[b][0:R, jj * 128:(jj + 1) * 128],
                                                               in_=src_fn(hf * 4 + jj), identity=ident[:])
                                                   for jj in range(4)],
                      reads=skeys + ["ident"], writes=(("ps", b),))
                P.add("dve", lambda e, b=b, hf=hf: e.tensor_copy(out=ncst[0:R, hf * 512:(hf + 1) * 512],
                                                                 in_=psb[b][0:R, 0:512]),
                      reads=(("ps", b),), writes=(("ncst", hf),))
            if sample:
                dst = ncs_d.ap()
            else:
                s_ = tl["s"]
                dst = ncp_d.ap()[2 * s_:2 * s_ + 2, :]
            P.add("pool", lambda e, dst=dst, R=R: e.dma_start(out=dst, in_=ncst[0:R, :]), reads=(("ncst", 0), ("ncst", 1)),
                  writes=("ncout",), slot="st_nc")

    def ffn(layer, TW):
        gu, dn = f"wgu{layer}", f"wdn{layer}"
        prenorm(layer * 4 + 2, TW)
        arena_switch()
        hk = [("h", k) for k in range(8)]
        for i in range(11):
            sg_, su_ = wload(gu, i), wload(gu, 11 + i)
            pro = proj_h_multi([(sg_, 0), (su_, 0), (sg_, 1), (su_, 1)], TW) if i == 0 else None
            for half in range(2):
                jf = 2 * i + half
                if pro is not None:
                    bg, bu = pro[2 * half:2 * half + 2]
                else:
                    bg = proj_fm(gu, i, half, sg_, lambda k: h[:, k, 0:TW], hk, TW)
                    bu = proj_fm(gu, 11 + i, half, su_, lambda k: h[:, k, 0:TW], hk, TW)
                st_ = sg_t[jf % 2]
                P.add("act", lambda e, bg=bg, st_=st_: e.activation(out=st_[:, 0:TW], in_=psb[bg][:, 0:TW], func=AF.Silu),
                      reads=(("ps", bg),) + AR, writes=(("sg_t", jf % 2),))
                P.add("dve", lambda e, bu=bu, st_=st_, jf=jf: e.tensor_tensor(out=a_act[:, jf, 0:TW], in0=psb[bu][:, 0:TW],
                                                                             in1=st_[:, 0:TW], op=ALU.mult),
                      reads=(("ps", bu), ("sg_t", jf % 2)) + AR, writes=(("a", jf),))

        def produce(n):
            s = wload(dn, n)
            b = P.bank()
            P.add("pe", lambda e: [e.matmul(psb[b][:, 0:TW], lhsT=wslot[:, s, k * 128:(k + 1) * 128],
                                            rhs=a_act[:, k, 0:TW], start=(k == 0), stop=(k == NFC - 1))
                                   for k in range(NFC)],
                  reads=[("a", k) for k in range(NFC)] + [("wslot", s)] + list(AR), writes=(("ps", b),))
            return b
        postnorm(layer * 4 + 3, TW, produce)

    SB = (4, 5, 6, 7)
    AB = (0, 1, 2, 3)

    def attention(tl):
        TW, sample = tl["TW"], tl["sample"]
        nb = TW // 128
        hk = [("h", k) for k in range(8)]
        arena_switch()
        half_cur = 0 if sample else tl["t"] % 2
        want_cache = sample or tl["last"]
        if not sample:
            P.add("pool", lambda e: e.memset(qTz[64:128, :, 0, :], 0.0), reads=AR, writes=("qz",))
            P.add("pool", lambda e: e.memset(qTz[0:64, :, 1, :], 0.0), reads=AR, writes=("qz",))
        def q_evac(c, b):
            if sample:
                evac_copy(qT[:, c, 0:TW], psb[b][:, 0:TW], reads=(("ps", b),) + AR, writes=(("qT", c),), scale=0.125)
            else:
                evac_copy(qTz[0:64, c, 0, 0:TW], psb[b][0:64, 0:TW], reads=(("ps", b), "qz") + AR,
                          writes=(("qT", c),), scale=0.125, eng="act")
                evac_copy(qTz[64:128, c, 1, 0:TW], psb[b][64:128, 0:TW], reads=(("ps", b), "qz") + AR,
                          writes=(("qT", c),), scale=0.125, eng="act")

        def k_evac(c, b):
            evac_copy(kT[:, c, half_cur * 512:half_cur * 512 + TW], psb[b][:, 0:TW], reads=(("ps", b),),
                      writes=(("kT", c, half_cur),))

        def k_tokmajor(i, s):
            for blk in range(nb):
                b = mmbank()
                P.add("pe", lambda e, b=b, blk=blk, s=s: [
                    e.matmul(psb[b][:, 0:256], lhsT=h[:, k, blk * 128:(blk + 1) * 128],
                             rhs=wslot[:, s, k * 256:(k + 1) * 256], start=(k == 0), stop=(k == 7))
                    for k in range(8)], reads=hk + [("wslot", s)], writes=(("ps", b),))
                mst = m[:].rearrange("p j t -> p (j t)")[:, blk * 1024 + i * 256: blk * 1024 + (i + 1) * 256]
                evac_copy(mst, psb[b][:, 0:256], reads=(("ps", b),), writes=(("m", blk * 2 + i // 2),))

        def k_store():
            mk = [("m", n) for n in range(8)]
            if sample:
                P.add("pool", lambda e: e.dma_start(out=ks_d.ap(), in_=m[:].rearrange("p j t -> p (j t)")[:, 0:1024]),
                      reads=mk, writes=("kout",), slot="st_k")
            else:
                dst = bass.AP(kp_d, tl["s"] * 512 * D, [[D, 128], [128 * D, 4], [1, D]])
                P.add("pool", lambda e, dst=dst: e.dma_start(out=dst, in_=m[:].rearrange("p (b j) t -> p b (j t)", b=4)),
                      reads=mk, writes=("kout",), slot="st_k")

        def v_proj(i):
            s = wload("wqkv", 8 + i)
            for blk in range(nb):
                b = mmbank()
                P.add("pe", lambda e, b=b, blk=blk, s=s: [
                    e.matmul(psb[b][:, 0:256], lhsT=h[:, k, blk * 128:(blk + 1) * 128],
                             rhs=wslot[:, s, k * 256:(k + 1) * 256], start=(k == 0), stop=(k == 7))
                    for k in range(8)], reads=hk + [("wslot", s)], writes=(("ps", b),))
                if not sample:
                    kb = half_cur * 4 + blk
                    vdst = Vb[:, kb, 2 * i:2 * i + 2, :].rearrange("p j (c d) -> p j c d", c=3)[:, :, 0:3:2, :]
                    evac_copy(vdst, psb[b][:, 0:256].rearrange("p (j c d) -> p j c d", j=2, c=2), reads=(("ps", b),),
                              writes=(("Vb", kb, i),), eng="act")
                if want_cache:
                    evac_copy(xin[:, blk, i * 256:(i + 1) * 256], psb[b][:, 0:256], reads=(("ps", b),),
                              writes=(("xin", blk),), eng="dve")
            if sample:
                for bs in range(4):
                    b = P.bank()
                    P.add("pe", lambda e, b=b, bs=bs, s=s: [
                        e.matmul(psb[b][0:32, 0:256], lhsT=h[:, k, bs * 32:(bs + 1) * 32],
                                 rhs=wslot[:, s, k * 256:(k + 1) * 256], start=(k == 0), stop=(k == 7))
                        for k in range(8)], reads=hk + [("wslot", s)], writes=(("ps", b),))
                    evac_copy(Vn[0:32, bs, i * 256:(i + 1) * 256], psb[b][0:32, 0:256], reads=(("ps", b),) + AR,
                              writes=(("Vn", bs, i),))

        def v_store():
            xk = [("xin", blk) for blk in range(nb)]
            if sample:
                P.add("pool", lambda e: e.dma_start(out=vs_d.ap(), in_=xin[:, 0, :]), reads=xk, writes=("vout",),
                      slot="st_v")
            else:
                dst = bass.AP(vp_d, tl["s"] * 512 * D, [[D, 128], [128 * D, 4], [1, D]])
                P.add("pool", lambda e, dst=dst: e.dma_start(out=dst, in_=xin[:]), reads=xk, writes=("vout",), slot="st_v")

        def qk_proj(i):
            sq_, sk_ = wload("wqkv", i), wload("wqkv", 4 + i)
            if i == 0:
                pro = proj_h_multi([(sq_, 0), (sq_, 1), (sk_, 0), (sk_, 1)], TW)
            for half in range(2):
                c = 2 * i + half
                b = pro[half] if i == 0 else proj_fm("wqkv", i, half, sq_, lambda k: h[:, k, 0:TW], hk, TW)
                q_evac(c, b)
            for half in range(2):
                c = 2 * i + half
                b = pro[2 + half] if i == 0 else proj_fm("wqkv", 4 + i, half, sk_, lambda k: h[:, k, 0:TW], hk, TW)
                k_evac(c, b)
            if want_cache:
                k_tokmajor(i, sk_)

        if sample:
            for i in range(4):
                qk_proj(i)
            k_store()
            for i in range(4):
                v_proj(i)
            v_store()

        def finish_pair(hp, bO, bD, c0, c1):
            rt = rec_t[hp % 2]
            P.add("dve", lambda e: e.reciprocal(out=rt[:, c0:c1], in_=psb[bD][:, c0:c1]), reads=(("ps", bD),) + AR,
                  writes=(("rec", hp % 2),))
            P.add("dve", lambda e: e.tensor_tensor(out=sqv[:, hp, c0:c1], in0=psb[bO][:, c0:c1], in1=rt[:, c0:c1],
                                                   op=ALU.mult),
                  reads=(("ps", bO), ("rec", hp % 2)) + AR, writes=(("sqv", hp),))

        def finish_head(hp, e_, bH):
            pb = 64 * e_
            po = 64 - pb
            rt = rec_t[e_]
            P.add("dve", lambda e: e.reciprocal(out=rt[pb:pb + 64, 0:TW], in_=psb[bH][po:po + 64, 0:TW]),
                  reads=(("ps", bH),) + AR, writes=(("rec", e_),))
            P.add("dve", lambda e: e.tensor_tensor(out=sqv[pb:pb + 64, hp, 0:TW], in0=psb[bH][pb:pb + 64, 0:TW],
                                                   in1=rt[pb:pb + 64, 0:TW], op=ALU.mult),
                  reads=(("ps", bH), ("rec", e_)) + AR, writes=(("sqv", hp),))

        if not sample:
            first = tl["first"]
            kbs = [4, 5, 6, 7] if first else [4, 0, 1, 2, 3, 5, 6, 7]
            hprev = 1 - half_cur
            LA = 3
            pending = []

            def flush(keep):
                while len(pending) > keep:
                    pending.pop(0)()
            def attend(hp):
                for e_ in range(2):
                    hd = 2 * hp + e_
                    bH = P.bank("att_acc", AB)
                    for ikb, kb in enumerate(kbs):
                        w0, w1 = 2 * kb, 2 * kb + 1
                        lo, hi = max(0, w0 - 8), min(7, w1)
                        nq = (hi - lo + 1) * 64
                        if kb < 4:
                            koff = hprev * 512 + kb * 128
                            kkeys = [("kT", hp, hprev)]
                            kbp = hprev * 4 + kb
                            corner = (0, (hi - lo) * 64)
                        else:
                            koff = half_cur * 512 + (kb - 4) * 128
                            kkeys = [("kT", hp, half_cur)]
                            kbp = half_cur * 4 + (kb - 4)
                            corner = (1, 0)
                        mlo = (lo + 8 - 2 * kb) * 64
                        nw = 0 if mlo >= 255 else min(nq, 256 - mlo)
                        bS = P.bank("att_s", SB)
                        pt_i = P.bank("att_pt", (0, 1, 2, 3))
                        pt = PT[pt_i]

                        def sfn_(e, bS=bS, hp=hp, e_=e_, koff=koff, lo=lo, hi=hi, nq=nq, hd=hd, mlo=mlo, nw=nw):
                            r = [e.matmul(psb[bS][:, 0:nq], lhsT=kT[:, hp, koff:koff + 128],
                                          rhs=qTz[:, hp, e_, lo * 64:(hi + 1) * 64], start=True, stop=(nw == 0),
                                          skip_group_check=True)]
                            if nw > 0:
                                r.append(e.matmul(psb[bS][:, 0:nw], lhsT=ident_bf[:], rhs=BT[:, hd, mlo:mlo + nw],
                                                  start=False, stop=True, skip_group_check=True))
                            return r
                        P.add("pe", sfn_, reads=kkeys + [("qT", hp), "qz", "ident_bf"] + list(BTK) + list(AR),
                              writes=(("ps", bS),))
                        P.add("act", lambda e, bS=bS, pt=pt, nq=nq: e.activation(out=pt[:, 0:nq], in_=psb[bS][:, 0:nq],
                                                                                 func=AF.Exp),
                              reads=(("ps", bS),) + AR, writes=(("PT", pt_i),))
                        cx, cc = corner
                        P.add("pool", lambda e, pt=pt, cx=cx, cc=cc: e.memset(pt[64 * cx:64 * cx + 64, cc:cc + 64], 0.0),
                              reads=AR, writes=(("PT", pt_i),))
                        vkeys = [("Vb", kbp, i) for i in range(4)]

                        def pvfn(e, bH=bH, pt=pt, kbp=kbp, hp=hp, e_=e_, lo=lo, hi=hi, nq=nq, ikb=ikb, nk=len(kbs)):
                            return [e.matmul(psb[bH][:, lo * 64:(hi + 1) * 64], lhsT=Vb[:, kbp, hp, e_ * 64:e_ * 64 + 128],
                                             rhs=pt[:, 0:nq], start=(ikb == 0), stop=(ikb == nk - 1),
                                             skip_group_check=True)]

                        def emit_pv(pvfn=pvfn, vkeys=vkeys, pt_i=pt_i, bH=bH, hp=hp, e_=e_, last=(ikb == len(kbs) - 1)):
                            P.add("pe", pvfn, reads=vkeys + [("PT", pt_i), "BTb"] + list(AR), writes=(("ps", bH),))
                            if last:
                                finish_head(hp, e_, bH)
                        pending.append(emit_pv)
                        flush(LA)
            for i in range(4):
                if i > 0:
                    mm_pool[0] = ("att_s", SB)
                qk_proj(i)
                v_proj(i)
                attend(2 * i)
                attend(2 * i + 1)
                flush(0)
            mm_pool[0] = None
            if want_cache:
                k_store()
                v_store()
        else:
            for bs in range(4):
                P.add("sp", lambda e, bs=bs: e.dma_start(
                    out=xin[:], in_=bass.AP(ck_d, bs * 512 * D, [[D, 128], [128 * D, 4], [1, D]])),
                    writes=[("xin", blk) for blk in range(4)], slot="ld_x")
                P.add("pool", lambda e, bs=bs: e.dma_start(
                    out=Vc, in_=bass.AP(cv_d, bs * 512 * D, [[D, 128], [128 * D, 4], [1, D]])),
                    reads=AR, writes=("Vc",), slot="ld_vc")
                for j in range(8):
                    b = P.bank("att_s", SB)
                    P.add("pe", lambda e, b=b, j=j: [e.transpose(out=psb[b][:, kb * 128:(kb + 1) * 128],
                                                                 in_=xin[:, kb, j * 128:(j + 1) * 128], identity=ident[:])
                                                     for kb in range(4)],
                          reads=[("xin", blk) for blk in range(4)] + ["ident"], writes=(("ps", b),))
                    evac_copy(kTc[:, j, :], psb[b][:, 0:512], reads=(("ps", b),) + AR, writes=(("kTc", j),))
                spend = []
                for hp in range(8):
                    bO = AB[(hp % 2) * 2]
                    bD = AB[(hp % 2) * 2 + 1]
                    for e_ in range(2):
                        hd = 2 * hp + e_
                        pb = 64 * e_
                        bS = P.bank("att_s", SB)
                        pt_i = P.bank("att_pt", (0, 1, 2, 3))
                        pt = PT[pt_i]

                        def sfn(e, bS=bS, pb=pb, hp=hp, hd=hd, bs=bs):
                            r = []
                            qa = qT[pb:pb + 64, hp, bs * 32:(bs + 1) * 32]
                            for kb in range(4):
                                r.append(e.matmul(psb[bS][:, kb * 32:(kb + 1) * 32],
                                                  lhsT=kTc[pb:pb + 64, hp, kb * 128:(kb + 1) * 128], rhs=qa,
                                                  start=True, stop=(512 - 128 * kb >= 255), skip_group_check=True,
                                                  tile_position=(pb, 0)))
                                mlo = 512 - 128 * kb
                                if mlo < 255:
                                    r.append(e.matmul(psb[bS][:, kb * 32:(kb + 1) * 32], lhsT=ident_bf[:],
                                                      rhs=BT[:, hd, mlo:mlo + 32], start=False, stop=True,
                                                      skip_group_check=True, tile_position=(0, 0)))
                            r.append(e.matmul(psb[bS][0:32, 128:160], lhsT=kT[pb:pb + 64, hp, bs * 32:(bs + 1) * 32],
                                              rhs=qa, start=True, stop=False, skip_group_check=True,
                                              tile_position=(pb, 0)))
                            r.append(e.matmul(psb[bS][0:32, 128:160], lhsT=ident_bf[0:32, 0:32], rhs=BT[0:32, hd, 0:32],
                                              start=False, stop=True, skip_group_check=True, tile_position=(0, 0)))
                            return r
                        P.add("pe", sfn, reads=[("kTc", hp), ("kT", hp, 0), ("qT", hp), "ident_bf"] + list(BTK) + list(AR),
                              writes=(("ps", bS),))
                        P.add("act", lambda e, bS=bS, pt=pt: e.activation(out=pt[:, 0:128], in_=psb[bS][:, 0:128],
                                                                          func=AF.Exp),
                              reads=(("ps", bS),) + AR, writes=(("PT", pt_i, "a"),))
                        P.add("act", lambda e, bS=bS, pt=pt: e.activation(out=pt[0:32, 128:160],
                                                                          in_=psb[bS][0:32, 128:160], func=AF.Exp),
                              reads=(("ps", bS),) + AR, writes=(("PT", pt_i, "b"),))

                        def pvs(e, bO=bO, bD=bD, pb=pb, pt=pt, hd=hd, bs=bs):
                            r = []
                            oo = psb[bO][pb:pb + 64, bs * 32:(bs + 1) * 32]
                            dd = psb[bD][pb:pb + 64, bs * 32:(bs + 1) * 32]
                            for kb in range(4):
                                r.append(e.matmul(oo, lhsT=Vc[:, kb, hd * 64:(hd + 1) * 64], rhs=pt[:, kb * 32:(kb + 1) * 32],
                                                  start=(kb == 0), stop=False, skip_group_check=True,
                                                  tile_position=(0, pb)))
                                r.append(e.matmul(dd, lhsT=ones_bf[:, 0:64], rhs=pt[:, kb * 32:(kb + 1) * 32],
                                                  start=(kb == 0), stop=False, skip_group_check=True,
                                                  tile_position=(0, pb)))
                            r.append(e.matmul(oo, lhsT=Vn[0:32, bs, hd * 64:(hd + 1) * 64], rhs=pt[0:32, 128:160],
                                              start=False, stop=True, skip_group_check=True, tile_position=(0, pb)))
                            r.append(e.matmul(dd, lhsT=ones_bf[0:32, 0:64], rhs=pt[0:32, 128:160],
                                              start=False, stop=True, skip_group_check=True, tile_position=(0, pb)))
                            return r
                        def emit_pvs(pvs=pvs, pt_i=pt_i, bs=bs, bO=bO, bD=bD, hp=hp, last=(e_ == 1)):
                            P.add("pe", pvs, reads=["Vc", ("PT", pt_i, "a"), ("PT", pt_i, "b"), "ones"]
                                  + [("Vn", bs, i) for i in range(4)] + list(AR),
                                  writes=(("ps", bO), ("ps", bD)))
                            if last:
                                finish_pair(hp, bO, bD, bs * 32, (bs + 1) * 32)
                        spend.append(emit_pvs)
                        while len(spend) > 2:
                            spend.pop(0)()
                while spend:
                    spend.pop(0)()

    tiles = []
    for s_ in range(4):
        for t_ in range(4):
            tiles.append(dict(s=s_, t=t_, TW=512, sample=False, first=(t_ == 0), last=(t_ == 3)))
    tiles = tiles[:n_prompt_tiles]
    if do_sample:
        tiles.append(dict(s=None, t=0, TW=128, sample=True, first=True, last=True))

    def load_x(tl):
        if tl["sample"]:
            P.add("sp", lambda e: e.dma_start(out=xin[:, 0, :], in_=xs_d.ap()), writes=(("xin", 0),), slot="ld_x")
        else:
            r0 = tl["s"] * SEQ + tl["t"] * 512
            src = bass.AP(xp_d, r0 * D, [[D, 128], [128 * D, 4], [1, D]])
            P.add("sp", lambda e: e.dma_start(out=xin[:], in_=src), writes=[("xin", blk) for blk in range(4)],
                  slot="ld_x")

    load_x(tiles[0])
    for ti, tl in enumerate(tiles):
        TW = tl["TW"]
        nb = TW // 128
        for j in range(8):
            b = P.bank()
            P.add("pe", lambda e, b=b, j=j, nb=nb: [e.transpose(out=psb[b][:, blk * 128:(blk + 1) * 128],
                                                                in_=xin[:, blk, j * 128:(j + 1) * 128], identity=ident[:])
                                                    for blk in range(nb)],
                  reads=[("xin", blk) for blk in range(nb)] + ["ident"], writes=(("ps", b),))
            evac_copy(x[:, j, 0:TW], psb[b][:, 0:TW], reads=(("ps", b),), writes=(("x", j),))
        nxt = tiles[ti + 1] if ti + 1 < len(tiles) else None
        prefetch_early = nxt is not None and not (do_l1 and (tl["last"] or tl["sample"]))
        checkpoint("xT")
        prenorm(0, TW)
        checkpoint("prenorm0")
        conv_mixer(tl)
        if prefetch_early:
            load_x(nxt)
        checkpoint("conv")
        postnorm(1, TW, out_proj("wout0", TW))
        checkpoint("post0")
        ffn(0, TW)
        checkpoint("ffn0")
        if do_l1:
            prenorm(4, TW)
            attention(tl)
            postnorm(5, TW, out_proj("wo", TW, kouter_banks=None if tl["sample"] else (4, 5, 6, 7)))
            if nxt is not None and not prefetch_early:
                load_x(nxt)
            ffn(1, TW)
        for blk in range(nb):
            for jg in range(2):
                b = P.bank()
                P.add("pe", lambda e, b=b, blk=blk, jg=jg: [
                    e.transpose(out=psb[b][:, jj * 128:(jj + 1) * 128], in_=x[:, jg * 4 + jj, blk * 128:(blk + 1) * 128],
                                identity=ident[:]) for jj in range(4)],
                    reads=[("x", jg * 4 + jj) for jj in range(4)] + ["ident"], writes=(("ps", b),))
                mst = m[:].rearrange("p j t -> p (j t)")[:, (blk * 2 + jg) * 512:(blk * 2 + jg + 1) * 512]
                evac_copy(mst, psb[b][:, 0:512], reads=(("ps", b),), writes=(("m", blk * 2 + jg),))
        mk = [("m", n) for n in range(2 * nb)]
        if tl["sample"]:
            P.add("pool", lambda e: e.dma_start(out=ys_d.ap(), in_=m[:].rearrange("p j t -> p (j t)")[:, 0:1024]),
                  reads=mk, writes=("yout",), slot="st_y")
        else:
            r0 = tl["s"] * SEQ + tl["t"] * 512
            dst = bass.AP(yp_d, r0 * D, [[D, 128], [128 * D, 4], [1, D]])
            P.add("pool", lambda e, dst=dst: e.dma_start(out=dst, in_=m[:].rearrange("p (b j) t -> p b (j t)", b=4)),
                  reads=mk, writes=("yout",), slot="st_y")
    return


def build_nc(**kw):
    nc = bass.Bass("TRN2", target_bir_lowering=False)
    with contextlib.ExitStack() as stack:
        P = build_program(nc, stack, **kw)
        per = P.finalize(nc, stack)
        block = stack.enter_context(nc.Block())
        names = {"pe": "tensor", "act": "scalar", "dve": "vector", "pool": "gpsimd", "sp": "sync"}
        for en in ("sp", "pool", "pe", "act", "dve"):
            ops = per.get(en, [])

            def body(eng, en=en, ops=ops):
                P.emit_engine(en, eng, ops, final_wait=(en == "sp"))
            getattr(block, names[en])(body)
    return nc


def make_in_maps(inputs):
    f = lambda a: np.ascontiguousarray(np.asarray(a, dtype=np.float32))
    norms = np.stack([inputs["norm_mix_pre"][0], inputs["norm_mix_post"][0], inputs["norm_ffn_pre"][0],
                      inputs["norm_ffn_post"][0], inputs["norm_mix_pre"][1], inputs["norm_mix_post"][1],
                      inputs["norm_ffn_pre"][1], inputs["norm_ffn_post"][1]], axis=0)
    shared = {
        "win0": f(inputs["conv_w_in"][0]), "wout0": f(inputs["conv_w_out"][0]),
        "wqkv": f(inputs["attn_w_qkv"][0]), "wo": f(inputs["attn_w_o"][0]),
        "wgu0": f(inputs["ffn_w_gate_up"][0]), "wgu1": f(inputs["ffn_w_gate_up"][1]),
        "wdn0": f(inputs["ffn_w_down"][0]), "wdn1": f(inputs["ffn_w_down"][1]),
        "convk": f(inputs["conv_kernel"][0]), "relb": f(inputs["attn_rel_bias"][0]),
        "norms": f(norms), "ident": np.eye(128, dtype=np.float32),
    }
    maps = []
    for c in range(NCORES):
        sl = slice(4 * c, 4 * c + 4)
        d = dict(shared)
        d["xp"] = f(inputs["x_prompt"][sl]).reshape(4 * SEQ, D)
        d["xs"] = f(inputs["x_sample"][sl]).reshape(128, D)
        d["sconv"] = f(inputs["state_conv"][0, sl]).reshape(8, D)
        d["ck"] = f(inputs["cache_k"][0, sl]).reshape(4 * 512, D)
        d["cv"] = f(inputs["cache_v"][0, sl]).reshape(4 * 512, D)
        maps.append(d)
    return maps


def assemble(results):
    cat = lambda k: np.concatenate([np.asarray(r[k], dtype=np.float32) for r in results], axis=0)
    yp = cat("yp").reshape(32, SEQ, D)
    ys = cat("ys").reshape(32, 32, D)
    ncp = cat("ncp").reshape(1, 32, 2, D)
    ncs = cat("ncs").reshape(1, 32, 2, D)
    kp = cat("kp").reshape(1, 32, 512, 16, 64)
    vp = cat("vp").reshape(1, 32, 512, 16, 64)
    ks = cat("ks").reshape(1, 32, 32, 16, 64)
    vs = cat("vs").reshape(1, 32, 32, 16, 64)
    return (yp, ys, ncp, ncs, kp, vp, ks, vs)


def kernel(**inputs):
    inputs = {k: np.asarray(v) for k, v in inputs.items()}
    nc = build_nc()
    res = run_bass_kernel_spmd(nc, make_in_maps(inputs), core_ids=list(range(NCORES)))
    return assemble(res.results)
```

```python
import contextlib
import numpy as np
import concourse.bass as bass
import concourse.mybir as mybir
from concourse.bass_utils import run_bass_kernel_spmd

F32 = mybir.dt.float32
BF16 = mybir.dt.bfloat16
AF = mybir.ActivationFunctionType
ALU = mybir.AluOpType

NCORES = 8
D = 1024
DFF = 2816
NFC = 22
SEQ = 2048
TWP = 512
NSLOT = 6
SLOTW = 2816
EPOCH = 12000
SAME_ENGINE_SYNC = True
EPS = 1e-6
ADD_ENG = "dve"


class Op:
    __slots__ = ("eng", "fn", "deps", "is_dma", "slot", "sem", "val", "signals", "idx", "ndma", "seq")


class Prog:
    def __init__(self):
        self.ops = []
        self.lastw = {}
        self.readers = {}
        self.dma_count = {}
        self.bank_rr = {}

    @staticmethod
    def _k(o):
        return ("dma", o.slot) if o.is_dma else ("eng", o.eng)

    def add(self, eng, fn, reads=(), writes=(), slot=None, ndma=1):
        op = Op()
        op.eng = eng
        op.fn = fn
        op.is_dma = slot is not None
        op.slot = slot
        op.ndma = ndma
        op.idx = len(self.ops)
        op.signals = False
        op.sem = None
        op.val = None
        deps = {}
        op_key_holder = [op]

        def dep(o):
            if o is None or o is op:
                return
            k = self._k(o)
            if k not in deps or deps[k].idx < o.idx:
                deps[k] = o

        for k in reads:
            dep(self.lastw.get(k))
            if isinstance(k, tuple) and k[0] == "ps":
                for rk, r in self.readers.get(k, {}).items():
                    if rk != self._k(op_key_holder[0]):
                        dep(r)
        for k in writes:
            dep(self.lastw.get(k))
            for r in self.readers.get(k, {}).values():
                dep(r)
        for k in writes:
            self.lastw[k] = op
            self.readers[k] = {}
        for k in reads:
            self.readers.setdefault(k, {})[self._k(op)] = op
        if op.is_dma:
            c = self.dma_count.get(slot, 0) + ndma
            self.dma_count[slot] = c
            op.val = 16 * c
        op.deps = []
        for k, o in deps.items():
            if k == ("eng", eng) and (eng == "pe" or not SAME_ENGINE_SYNC):
                continue
            op.deps.append(o)
            o.signals = True
        self.ops.append(op)
        return op

    def finalize(self, nc, stack):
        per = {}
        for op in self.ops:
            per.setdefault(op.eng, []).append(op)
        self.eng_sems = {}
        for eng, ops in per.items():
            n = 0
            for op in ops:
                if op.is_dma or not op.signals:
                    continue
                op.seq = n
                n += 1
            nep = (n + EPOCH - 1) // EPOCH
            sems = [stack.enter_context(nc.semaphore(f"s_{eng}_{i}")) for i in range(max(nep, 1))]
            self.eng_sems[eng] = sems
            for op in ops:
                if op.is_dma or not op.signals:
                    continue
                op.sem = sems[op.seq // EPOCH]
                op.val = op.seq % EPOCH + 1
        self.dma_sems = {}
        for slot in self.dma_count:
            self.dma_sems[slot] = stack.enter_context(nc.semaphore(f"d_{slot}"))
        for op in self.ops:
            if op.is_dma:
                op.sem = self.dma_sems[op.slot]
        return per

    def emit_engine(self, eng_name, eng, ops, final_wait=False):
        known = {}
        for op in ops:
            for d in op.deps:
                key = id(d.sem)
                if known.get(key, 0) >= d.val:
                    continue
                eng.wait_ge(d.sem, d.val)
                known[key] = d.val
            r = op.fn(eng)
            if op.is_dma:
                insts = r if isinstance(r, (list, tuple)) else [r]
                assert len(insts) == op.ndma, (len(insts), op.ndma)
                for ins in insts:
                    ins.then_inc(op.sem, 16)
            else:
                insts = r if isinstance(r, (list, tuple)) else [r]
                if op.signals:
                    insts[-1].then_inc(op.sem, 1)
        if final_wait:
            for slot, sem in self.dma_sems.items():
                eng.wait_ge(sem, 16 * self.dma_count[slot])

    def bank(self, pool="mm", banks=(0, 1, 2, 3, 4, 5, 6, 7)):
        i = self.bank_rr.get(pool, 0)
        self.bank_rr[pool] = i + 1
        return banks[i % len(banks)]


class _Stop(Exception):
    pass


def build_program(nc, stack, n_prompt_tiles=16, do_sample=True, do_l1=True, stop=None, skip=()):
    P = Prog()

    def checkpoint(name):
        if stop == name:
            raise _Stop()
    try:
        _build_body(nc, stack, P, n_prompt_tiles, do_sample, do_l1, checkpoint, skip)
    except _Stop:
        pass
    return P


def _build_body(nc, stack, P, n_prompt_tiles, do_sample, do_l1, checkpoint, skip):

    def din(name, shape):
        return nc.dram_tensor(name, list(shape), F32, kind="ExternalInput")

    def dout(name, shape):
        return nc.dram_tensor(name, list(shape), F32, kind="ExternalOutput")

    xp_d = din("xp", [4 * SEQ, D])
    xs_d = din("xs", [128, D])
    sconv_d = din("sconv", [8, D])
    ck_d = din("ck", [4 * 512, D])
    cv_d = din("cv", [4 * 512, D])
    w_d = {
        "win0": din("win0", [D, 3 * D]), "wout0": din("wout0", [D, D]),
        "wqkv": din("wqkv", [D, 3 * D]), "wo": din("wo", [D, D]),
        "wgu0": din("wgu0", [D, 2 * DFF]), "wgu1": din("wgu1", [D, 2 * DFF]),
        "wdn0": din("wdn0", [DFF, D]), "wdn1": din("wdn1", [DFF, D]),
    }
    convk_d = din("convk", [3, D])
    relb_d = din("relb", [16, 257])
    norms_d = din("norms", [8, D])
    ident_d = din("ident", [128, 128])

    yp_d = dout("yp", [4 * SEQ, D])
    ys_d = dout("ys", [128, D])
    ncp_d = dout("ncp", [8, D])
    ncs_d = dout("ncs", [8, D])
    kp_d = dout("kp", [4 * 512, D])
    vp_d = dout("vp", [4 * 512, D])
    ks_d = dout("ks", [128, D])
    vs_d = dout("vs", [128, D])

    wscr = {}
    for name, t in w_d.items():
        if name.startswith("wdn"):
            wscr[name] = nc.dram_tensor("s_" + name, [8, 128, NFC * 128], BF16, kind="Internal")
        else:
            C = t.shape[1]
            wscr[name] = nc.dram_tensor("s_" + name, [C // 256, 128, 8 * 256], BF16, kind="Internal")
    r2_d = nc.dram_tensor("s_r2", [128, 16 * 383], F32, kind="Internal")

    def sb(name, shape, dt):
        return stack.enter_context(nc.sbuf_tensor("sbt_" + name, list(shape), dt))

    x = sb("x", [128, 8, 512], F32)
    xin = sb("xin", [128, 4, 1024], F32)
    h = sb("h", [128, 8, 512], BF16)
    sqv = sb("sqv", [128, 8, 512], BF16)
    sq = sb("sq", [128, 8, 512], BF16)
    m = sb("m", [128, 8, 512], F32)
    rs = sb("rs", [128, 512], F32)
    rstd = sb("rstd", [128, 512], F32)
    kT = sb("kT", [128, 8, 1024], BF16)
    Vb = sb("Vb", [128, 8, 8, 192], BF16)
    wslot = sb("wslot", [128, NSLOT, SLOTW], BF16)
    BT = sb("BT", [128, 16, 256], BF16)
    ident = sb("ident_sb", [128, 128], F32)
    ident_bf = sb("ident_bf", [128, 128], BF16)
    ones_bf = sb("ones_bf", [128, 128], BF16)
    gT = sb("gT", [128, 8, 8], F32)
    cwT = sb("cwT", [128, 8, 3], F32)
    carry = sb("carry", [128, 8, 2], F32)
    small_in = sb("small_in", [8, 1024], F32)
    ncst = sb("ncst", [8, 1024], F32)
    small_in2 = ncst
    eps_t = sb("eps_t", [128, 1], F32)
    chb = sb("chb", [128, 16], F32)
    dummy = sb("dummy_t", [128, 2], F32)
    ARENA = 10240
    arena = sb("arena", [128, ARENA], F32)

    psb = [stack.enter_context(nc.psum_tensor(f"ps{i}", [128, 512], F32)) for i in range(8)]

    def a_f32(off, n):
        return arena[:, off:off + n]

    def a_bf16(off, n_bf):
        return arena[:, off:off + n_bf // 2].bitcast(BF16)

    u_ext = a_f32(0, 8 * 514).rearrange("p (j t) -> p j t", j=8)
    u_ext_s = a_f32(0, 8 * 136).rearrange("p (j s t) -> p j s t", j=8, s=4)
    xh_t = [a_f32(4112 + i * 512, 512) for i in range(2)]
    y_t = [a_f32(4112 + 1024 + i * 512, 512) for i in range(2)]
    ul_s = a_f32(4112 + 2048, 64).rearrange("p (j s r) -> p j s r", j=8, s=4)
    a_act = a_bf16(0, NFC * 512).rearrange("p (j t) -> p j t", j=NFC)
    sg_t = [a_f32(5632 + i * 512, 512) for i in range(2)]
    qT = a_bf16(0, 8 * 512).rearrange("p (j t) -> p j t", j=8)
    qTz = a_bf16(0, 8 * 2 * 512).rearrange("p (j e t) -> p j e t", j=8, e=2)
    Vn = a_bf16(2048, 4 * 1024).rearrange("p (b c) -> p b c", b=4)
    PT = [a_bf16(4096 + i * 256, 512) for i in range(4)]
    rec_t = [a_f32(5120 + i * 512, 512) for i in range(2)]
    kTc = a_bf16(6144, 8 * 512).rearrange("p (j t) -> p j t", j=8)
    Vc = a_bf16(8192, 4 * 1024).rearrange("p (b c) -> p b c", b=4)
    tabs = a_f32(0, 16 * 257).rearrange("p (h e) -> p h e", h=16)
    E2 = a_f32(4112, 16 * 383).rearrange("p (h e) -> p h e", h=16)

    phase_ctr = [0]

    def arena_switch():
        P.add("dve", lambda e: e.memset(dummy[:, 0:1], 0.0), reads=(), writes=("arena_phase",))

    AR = ("arena_phase",)

    P.add("sp", lambda e: e.dma_start(out=ident[:], in_=ident_d.ap()), writes=("ident",), slot="ld_ident")
    P.add("sp", lambda e: e.dma_start(out=small_in[:], in_=norms_d.ap()), writes=("small_in",), slot="ld_small")
    P.add("sp", lambda e: e.dma_start(out=small_in2[0:3, :], in_=convk_d.ap()), writes=(("ncst", 0), ("ncst", 1)), slot="ld_small2")
    P.add("sp", lambda e: e.dma_start(out=tabs, in_=bass.AP(relb_d, 0, [[0, 128], [257, 16], [1, 257]])),
          reads=AR, writes=("tabs",), slot="ld_tabs")
    P.add("dve", lambda e: e.memset(ones_bf[:], 1.0), writes=("ones",))
    P.add("dve", lambda e: e.memset(eps_t[:], EPS), writes=("eps",))
    P.add("dve", lambda e: e.tensor_copy(out=ident_bf[:], in_=ident[:]), reads=("ident",), writes=("ident_bf",))

    checkpoint("setup0")
    checkpoint("convw")

    b = P.bank()
    P.add("pe", lambda e, b=b: [e.transpose(out=psb[b][:, j * 8:(j + 1) * 8], in_=small_in[0:8, j * 128:(j + 1) * 128],
                                            identity=ident[0:8, 0:8]) for j in range(8)],
          reads=("small_in", "ident"), writes=(("ps", b),))
    P.add("dve", lambda e, b=b: e.tensor_copy(out=gT[:].rearrange("p j g -> p (j g)"), in_=psb[b][:, 0:64]),
          reads=(("ps", b),), writes=("gT",))
    b = P.bank()
    P.add("pe", lambda e, b=b: [e.transpose(out=psb[b][:, j * 3:(j + 1) * 3], in_=small_in2[0:3, j * 128:(j + 1) * 128],
                                            identity=ident[0:3, 0:3]) for j in range(8)],
          reads=(("ncst", 0), ("ncst", 1), "ident"), writes=(("ps", b),))
    P.add("dve", lambda e, b=b: e.tensor_copy(out=cwT[:].rearrange("p j g -> p (j g)"), in_=psb[b][:, 0:24]),
          reads=(("ps", b),), writes=("cwT",))

    checkpoint("gains")
    P.add("dve", lambda e: e.tensor_copy(out=chb[:].rearrange("p (h o) -> p h o", o=1), in_=tabs[:, :, 256:257]),
          reads=("tabs",) + AR, writes=("chb",))
    P.add("dve", lambda e: e.tensor_tensor(out=E2[:, :, 0:256], in0=tabs[:, :, 1:257],
                                           in1=chb[:].rearrange("p (h o) -> p h o", o=1).broadcast_to([128, 16, 256]),
                                           op=ALU.subtract),
          reads=("tabs", "chb") + AR, writes=("E2a",))
    P.add("dve", lambda e: e.memset(E2[:, :, 256:383], 0.0), reads=AR, writes=("E2b",))
    P.add("dve", lambda e: e.memset(Vb[:, :, :, 64:128], 1.0), reads=(), writes=("BTb",))
    P.add("pool", lambda e: e.dma_start(out=r2_d.ap(), in_=E2.rearrange("p h e -> p (h e)")),
          reads=("E2a", "E2b") + AR, writes=("r2",), slot="st_r2")
    P.add("pool", lambda e: e.dma_start(out=BT[:, :, 0:256],
                                        in_=bass.AP(r2_d, 127, [[16 * 383 - 1, 128], [383, 16], [1, 256]])),
          reads=("r2",), writes=("BTa",), slot="ld_bt")
    BTK = ("BTa", "BTb")
    checkpoint("bt")

    wstate = {"i": 0}

    converted = set()

    def wload(name, piece):
        s = wstate["i"] % NSLOT
        wstate["i"] += 1
        t = w_d[name]
        if name.startswith("wdn"):
            dst = wslot[:, s, 0:NFC * 128]
            src32 = bass.AP(t, piece * 128, [[D, 128], [128 * D, NFC], [1, 128]])
            dst3 = dst.rearrange("p (k c) -> p k c", k=NFC)
        else:
            C = t.shape[1]
            dst = wslot[:, s, 0:2048]
            src32 = bass.AP(t, piece * 256, [[C, 128], [128 * C, 8], [1, 256]])
            dst3 = dst.rearrange("p (k c) -> p k c", k=8)
        scr = wscr[name].ap()[piece]
        key = ("wscr", name, piece)
        if key not in converted:
            converted.add(key)
            P.add("pool", lambda e: e.dma_start(out=dst3, in_=src32), writes=(("wslot", s),), slot=f"wc{s}")
            P.add("sp", lambda e: e.dma_start(out=scr, in_=dst), reads=(("wslot", s),), writes=(key,), slot=f"wb{s}")
        else:
            P.add("sp", lambda e: e.dma_start(out=dst, in_=scr), reads=(key,), writes=(("wslot", s),), slot=f"w{s}")
        return s

    evac_rr = [0]

    def evac_copy(out_ap, in_ap, reads, writes, scale=None, eng=None):
        if eng is None:
            eng = "act" if evac_rr[0] % 2 == 0 else "dve"
            evac_rr[0] += 1
        if eng == "act":
            if scale is None:
                P.add("act", lambda e: e.activation(out=out_ap, in_=in_ap, func=AF.Copy), reads=reads, writes=writes)
            else:
                P.add("act", lambda e: e.activation(out=out_ap, in_=in_ap, func=AF.Copy, scale=scale),
                      reads=reads, writes=writes)
        else:
            if scale is None:
                P.add("dve", lambda e: e.tensor_copy(out=out_ap, in_=in_ap), reads=reads, writes=writes)
            else:
                P.add("dve", lambda e: e.tensor_scalar(out=out_ap, in0=in_ap, scalar1=scale, scalar2=None,
                                                       op0=ALU.mult), reads=reads, writes=writes)

    def norm_stats(TW):
        b = P.bank()
        for j in range(8):
            P.add("pe", lambda e, j=j: e.matmul(psb[b][:, 0:TW], lhsT=ones_bf[:], rhs=sq[:, j, 0:TW],
                                                start=(j == 0), stop=(j == 7)),
                  reads=[("sq", j), "ones"], writes=(("ps", b),))
        P.add("act", lambda e: e.activation(out=rs[:, 0:TW], in_=psb[b][:, 0:TW], func=AF.Ln,
                                            bias=eps_t[:, 0:1], scale=1.0 / D),
              reads=(("ps", b), "eps"), writes=("rs",))
        P.add("act", lambda e: e.activation(out=rstd[:, 0:TW], in_=rs[:, 0:TW], func=AF.Exp, scale=-0.5),
              reads=("rs",), writes=("rstd",))

    def prenorm(g, TW):
        for j in range(8):
            P.add("act", lambda e, j=j: e.activation(out=sq[:, j, 0:TW], in_=x[:, j, 0:TW], func=AF.Square),
                  reads=(("x", j),), writes=(("sq", j),))
        norm_stats(TW)
        for j in range(8):
            P.add("dve", lambda e, j=j: e.scalar_tensor_tensor(out=h[:, j, 0:TW], in0=x[:, j, 0:TW],
                                                               scalar=gT[:, j, g:g + 1], in1=rstd[:, 0:TW],
                                                               op0=ALU.mult, op1=ALU.mult),
                  reads=(("x", j), "rstd", "gT"), writes=(("h", j),))

    def postnorm(g, TW, produce):
        for n in range(8):
            b = produce(n)
            checkpoint("pn_mm")
            P.add("act", lambda e, b=b, n=n: e.activation(out=sq[:, n, 0:TW], in_=psb[b][:, 0:TW], func=AF.Square),
                  reads=(("ps", b),), writes=(("sq", n),))
            checkpoint("pn_sq")
            P.add("dve", lambda e, b=b, n=n: e.tensor_copy(out=m[:, n, 0:TW], in_=psb[b][:, 0:TW]),
                  reads=(("ps", b),), writes=(("m", n),))
            checkpoint("pn_a%d" % n)
        checkpoint("pn_a")
        norm_stats(TW)
        checkpoint("pn_b")
        def stt(n):
            P.add("dve", lambda e, n=n: e.scalar_tensor_tensor(out=m[:, n, 0:TW], in0=m[:, n, 0:TW],
                                                               scalar=gT[:, n, g:g + 1], in1=rstd[:, 0:TW],
                                                               op0=ALU.mult, op1=ALU.mult),
                  reads=(("m", n), "rstd", "gT"), writes=(("m", n),))

        def add(n):
            P.add(ADD_ENG, lambda e, n=n: e.tensor_tensor(out=x[:, n, 0:TW], in0=x[:, n, 0:TW], in1=m[:, n, 0:TW],
                                                         op=ALU.add),
                  reads=(("x", n), ("m", n)), writes=(("x", n),))
        stt(0)
        for n in range(8):
            if n + 1 < 8:
                stt(n + 1)
            add(n)

    mm_pool = [None]

    def mmbank():
        return P.bank(*mm_pool[0]) if mm_pool[0] else P.bank()

    def proj_fm(name, piece, half, s, rhs_fn, rhs_keys, TW, nk=8):
        b = mmbank()
        P.add("pe", lambda e: [e.matmul(psb[b][:, 0:TW], lhsT=wslot[:, s, k * 256 + half * 128:k * 256 + half * 128 + 128],
                                        rhs=rhs_fn(k), start=(k == 0), stop=(k == nk - 1)) for k in range(nk)],
              reads=list(rhs_keys) + [("wslot", s)], writes=(("ps", b),))
        return b

    def proj_h_multi(groups, TW):
        banks = [P.bank() for _ in groups]
        for k in range(8):
            P.add("pe", lambda e, k=k: [e.matmul(psb[b][:, 0:TW],
                                                 lhsT=wslot[:, s, k * 256 + half * 128:k * 256 + half * 128 + 128],
                                                 rhs=h[:, k, 0:TW], start=(k == 0), stop=(k == 7))
                                        for b, (s, half) in zip(banks, groups)],
                  reads=[("h", k)] + [("wslot", s) for s, _ in groups], writes=[("ps", b) for b in banks])
        return banks

    def out_proj(name, TW, kouter_banks=None):
        st = {}

        def produce(n):
            piece, half = n // 2, n % 2
            if kouter_banks is not None and n < 4:
                if n == 0:
                    sl = [wload(name, 0), wload(name, 1)]
                    groups = [(sl[0], 0), (sl[0], 1), (sl[1], 0), (sl[1], 1)]
                    st["kb"] = list(kouter_banks)
                    for k in range(8):
                        P.add("pe", lambda e, k=k, groups=groups: [
                            e.matmul(psb[b][:, 0:TW], lhsT=wslot[:, s, k * 256 + hf * 128:k * 256 + hf * 128 + 128],
                                     rhs=sqv[:, k, 0:TW], start=(k == 0), stop=(k == 7))
                            for b, (s, hf) in zip(st["kb"], groups)],
                            reads=[("sqv", k)] + [("wslot", s) for s, _ in groups],
                            writes=[("ps", b) for b in st["kb"]])
                return st["kb"][n]
            if half == 0:
                st["s"] = wload(name, piece)
            return proj_fm(name, piece, half, st["s"], lambda k: sqv[:, k, 0:TW], [("sqv", k) for k in range(8)], TW)
        return produce

    def conv_mixer(tl):
        TW, sample = tl["TW"], tl["sample"]
        arena_switch()
        if sample:
            P.add("sp", lambda e: e.dma_start(out=small_in[:], in_=sconv_d.ap()), writes=("small_in",), slot="ld_small")
            b = P.bank()
            P.add("pe", lambda e, b=b: [e.transpose(out=psb[b][:, j * 8:(j + 1) * 8],
                                                    in_=small_in[0:8, j * 128:(j + 1) * 128], identity=ident[0:8, 0:8])
                                        for j in range(8)],
                  reads=("small_in", "ident"), writes=(("ps", b),))
            P.add("dve", lambda e, b=b: e.tensor_copy(out=u_ext_s[:, :, :, 0:2],
                                                 in_=psb[b][:, 0:64].rearrange("p (j s r) -> p j s r", j=8, s=4)),
                  reads=(("ps", b),) + AR, writes=[("u", j) for j in range(8)])
        elif tl["first"]:
            P.add("dve", lambda e: e.memset(u_ext[:, :, 0:2], 0.0), reads=AR, writes=[("u", j) for j in range(8)])
        else:
            P.add("dve", lambda e: e.tensor_copy(out=u_ext[:, :, 0:2], in_=carry[:]),
                  reads=("carry",) + AR, writes=[("u", j) for j in range(8)])

        def v3(ap):
            return ap.rearrange("p (s t) -> p s t", s=4) if sample else ap

        def utap(jc, k):
            return u_ext_s[:, jc, :, k:k + 32] if sample else u_ext[:, jc, k:k + TW]

        for i in range(4):
            sl = [wload("win0", i), wload("win0", 4 + i), wload("win0", 8 + i)]
            pro = proj_h_multi([(sl[0], 0), (sl[1], 0), (sl[2], 0), (sl[0], 1), (sl[1], 1), (sl[2], 1)], TW) if i == 0 else None
            for half in range(2):
                jc = 2 * i + half
                hk = [("h", k) for k in range(8)]
                if pro is not None:
                    bb, bc, bx = pro[3 * half:3 * half + 3]
                else:
                    bb = proj_fm("win0", i, half, sl[0], lambda k: h[:, k, 0:TW], hk, TW)
                    bc = proj_fm("win0", 4 + i, half, sl[1], lambda k: h[:, k, 0:TW], hk, TW)
                    bx = proj_fm("win0", 8 + i, half, sl[2], lambda k: h[:, k, 0:TW], hk, TW)
                xt = xh_t[jc % 2]
                yt = y_t[jc % 2]
                P.add("act", lambda e, bx=bx, xt=xt: e.activation(out=xt[:, 0:TW], in_=psb[bx][:, 0:TW], func=AF.Copy),
                      reads=(("ps", bx),) + AR, writes=(("xh_t", jc % 2),))
                P.add("dve", lambda e, bc=bc, xt=xt, jc=jc: e.tensor_tensor(out=utap(jc, 2), in0=v3(psb[bc][:, 0:TW]),
                                                                           in1=v3(xt[:, 0:TW]), op=ALU.mult),
                      reads=(("ps", bc), ("xh_t", jc % 2)) + AR, writes=(("u", jc),))
                P.add("pool", lambda e, jc=jc, yt=yt: e.tensor_scalar(out=v3(yt[:, 0:TW]), in0=utap(jc, 0),
                                                                      scalar1=cwT[:, jc, 0:1], scalar2=0.0,
                                                                      op0=ALU.mult, op1=ALU.add),
                      reads=(("u", jc), "cwT") + AR, writes=(("y_t", jc % 2),))
                for k in (1, 2):
                    P.add("dve", lambda e, jc=jc, yt=yt, k=k: e.scalar_tensor_tensor(
                        out=v3(yt[:, 0:TW]), in0=utap(jc, k), scalar=cwT[:, jc, k:k + 1], in1=v3(yt[:, 0:TW]),
                        op0=ALU.mult, op1=ALU.add),
                        reads=(("u", jc), "cwT", ("y_t", jc % 2)) + AR, writes=(("y_t", jc % 2),))
                P.add("dve", lambda e, bb=bb, jc=jc, yt=yt: e.tensor_tensor(out=sqv[:, jc, 0:TW], in0=psb[bb][:, 0:TW],
                                                                           in1=yt[:, 0:TW], op=ALU.mult),
                      reads=(("ps", bb), ("y_t", jc % 2)) + AR, writes=(("sqv", jc),))
        uk = [("u", j) for j in range(8)]
        if sample:
            P.add("dve", lambda e: e.tensor_copy(out=ul_s, in_=u_ext_s[:, :, :, 32:34]), reads=uk + list(AR),
                  writes=("ul_s",))
            R = 8
            src_fn = lambda jc: ul_s[:, jc, :, :].rearrange("p s r -> p (s r)")
            skeys = ["ul_s"] + list(AR)
        else:
            P.add("dve", lambda e: e.tensor_copy(out=carry[:], in_=u_ext[:, :, TW:TW + 2]), reads=uk + list(AR),
                  writes=("carry",))
            R = 2
            src_fn = lambda jc: carry[:, jc, :]
            skeys = ["carry"]
        if sample or tl["last"]:
            for hf in range(2):
                b = P.bank()
                P.add("pe", lambda e, b=b, hf=hf: [e.transpose(out=psb[b][0:R, jj * 128:(jj + 1) * 128],
                                                               in_=src_fn(hf * 4 + jj), identity=ident[:])
                                                   for jj in range(4)],
                      reads=skeys + ["ident"], writes=(("ps", b),))
                P.add("dve", lambda e, b=b, hf=hf: e.tensor_copy(out=ncst[0:R, hf * 512:(hf + 1) * 512],
                                                                 in_=psb[b][0:R, 0:512]),
                      reads=(("ps", b),), writes=(("ncst", hf),))
            if sample:
                dst = ncs_d.ap()
            else:
                s_ = tl["s"]
                dst = ncp_d.ap()[2 * s_:2 * s_ + 2, :]
            P.add("pool", lambda e, dst=dst, R=R: e.dma_start(out=dst, in_=ncst[0:R, :]), reads=(("ncst", 0), ("ncst", 1)),
                  writes=("ncout",), slot="st_nc")

    def ffn(layer, TW):
        gu, dn = f"wgu{layer}", f"wdn{layer}"
        prenorm(layer * 4 + 2, TW)
        arena_switch()
        hk = [("h", k) for k in range(8)]
        for i in range(11):
            sg_, su_ = wload(gu, i), wload(gu, 11 + i)
            pro = proj_h_multi([(sg_, 0), (su_, 0), (sg_, 1), (su_, 1)], TW) if i == 0 else None
            for half in range(2):
                jf = 2 * i + half
                if pro is not None:
                    bg, bu = pro[2 * half:2 * half + 2]
                else:
                    bg = proj_fm(gu, i, half, sg_, lambda k: h[:, k, 0:TW], hk, TW)
                    bu = proj_fm(gu, 11 + i, half, su_, lambda k: h[:, k, 0:TW], hk, TW)
                st_ = sg_t[jf % 2]
                P.add("act", lambda e, bg=bg, st_=st_: e.activation(out=st_[:, 0:TW], in_=psb[bg][:, 0:TW], func=AF.Silu),
                      reads=(("ps", bg),) + AR, writes=(("sg_t", jf % 2),))
                P.add("dve", lambda e, bu=bu, st_=st_, jf=jf: e.tensor_tensor(out=a_act[:, jf, 0:TW], in0=psb[bu][:, 0:TW],
                                                                             in1=st_[:, 0:TW], op=ALU.mult),
                      reads=(("ps", bu), ("sg_t", jf % 2)) + AR, writes=(("a", jf),))

        dst_ = {}

        def produce(n):
            if n < 4:
                if n == 0:
                    sl = [wload(dn, i) for i in range(4)]
                    dst_["b"] = [P.bank() for _ in range(4)]
                    for k in range(NFC):
                        P.add("pe", lambda e, k=k, sl=sl: [
                            e.matmul(psb[b][:, 0:TW], lhsT=wslot[:, s, k * 128:(k + 1) * 128], rhs=a_act[:, k, 0:TW],
                                     start=(k == 0), stop=(k == NFC - 1)) for b, s in zip(dst_["b"], sl)],
                            reads=[("a", k)] + [("wslot", s) for s in sl] + list(AR),
                            writes=[("ps", b) for b in dst_["b"]])
                return dst_["b"][n]
            s = wload(dn, n)
            b = P.bank()
            P.add("pe", lambda e: [e.matmul(psb[b][:, 0:TW], lhsT=wslot[:, s, k * 128:(k + 1) * 128],
                                            rhs=a_act[:, k, 0:TW], start=(k == 0), stop=(k == NFC - 1))
                                   for k in range(NFC)],
                  reads=[("a", k) for k in range(NFC)] + [("wslot", s)] + list(AR), writes=(("ps", b),))
            return b
        postnorm(layer * 4 + 3, TW, produce)

    SB = (4, 5, 6, 7)
    AB = (0, 1, 2, 3)

    def attention(tl):
        TW, sample = tl["TW"], tl["sample"]
        nb = TW // 128
        hk = [("h", k) for k in range(8)]
        arena_switch()
        half_cur = 0 if sample else tl["t"] % 2
        want_cache = sample or tl["last"]
        if not sample:
            P.add("pool", lambda e: e.memset(qTz[64:128, :, 0, :], 0.0), reads=AR, writes=("qz",))
            P.add("pool", lambda e: e.memset(qTz[0:64, :, 1, :], 0.0), reads=AR, writes=("qz",))
        def q_evac(c, b):
            if sample:
                evac_copy(qT[:, c, 0:TW], psb[b][:, 0:TW], reads=(("ps", b),) + AR, writes=(("qT", c),), scale=0.125)
            else:
                evac_copy(qTz[0:64, c, 0, 0:TW], psb[b][0:64, 0:TW], reads=(("ps", b), "qz") + AR,
                          writes=(("qT", c),), scale=0.125, eng="act")
                evac_copy(qTz[64:128, c, 1, 0:TW], psb[b][64:128, 0:TW], reads=(("ps", b), "qz") + AR,
                          writes=(("qT", c),), scale=0.125, eng="act")

        def k_evac(c, b):
            evac_copy(kT[:, c, half_cur * 512:half_cur * 512 + TW], psb[b][:, 0:TW], reads=(("ps", b),),
                      writes=(("kT", c, half_cur),))

        def k_tokmajor(i, s):
            for blk in range(nb):
                b = mmbank()
                P.add("pe", lambda e, b=b, blk=blk, s=s: [
                    e.matmul(psb[b][:, 0:256], lhsT=h[:, k, blk * 128:(blk + 1) * 128],
                             rhs=wslot[:, s, k * 256:(k + 1) * 256], start=(k == 0), stop=(k == 7))
                    for k in range(8)], reads=hk + [("wslot", s)], writes=(("ps", b),))
                mst = m[:].rearrange("p j t -> p (j t)")[:, blk * 1024 + i * 256: blk * 1024 + (i + 1) * 256]
                evac_copy(mst, psb[b][:, 0:256], reads=(("ps", b),), writes=(("m", blk * 2 + i // 2),))

        def k_store():
            mk = [("m", n) for n in range(8)]
            if sample:
                P.add("pool", lambda e: e.dma_start(out=ks_d.ap(), in_=m[:].rearrange("p j t -> p (j t)")[:, 0:1024]),
                      reads=mk, writes=("kout",), slot="st_k")
            else:
                dst = bass.AP(kp_d, tl["s"] * 512 * D, [[D, 128], [128 * D, 4], [1, D]])
                P.add("pool", lambda e, dst=dst: e.dma_start(out=dst, in_=m[:].rearrange("p (b j) t -> p b (j t)", b=4)),
                      reads=mk, writes=("kout",), slot="st_k")

        def v_proj(i):
            s = wload("wqkv", 8 + i)
            for blk in range(nb):
                b = mmbank()
                P.add("pe", lambda e, b=b, blk=blk, s=s: [
                    e.matmul(psb[b][:, 0:256], lhsT=h[:, k, blk * 128:(blk + 1) * 128],
                             rhs=wslot[:, s, k * 256:(k + 1) * 256], start=(k == 0), stop=(k == 7))
                    for k in range(8)], reads=hk + [("wslot", s)], writes=(("ps", b),))
                if not sample:
                    kb = half_cur * 4 + blk
                    vdst = Vb[:, kb, 2 * i:2 * i + 2, :].rearrange("p j (c d) -> p j c d", c=3)[:, :, 0:3:2, :]
                    evac_copy(vdst, psb[b][:, 0:256].rearrange("p (j c d) -> p j c d", j=2, c=2), reads=(("ps", b),),
                              writes=(("Vb", kb, i),), eng="act")
                if want_cache:
                    evac_copy(xin[:, blk, i * 256:(i + 1) * 256], psb[b][:, 0:256], reads=(("ps", b),),
                              writes=(("xin", blk),), eng="dve")
            if sample:
                for bs in range(4):
                    b = P.bank()
                    P.add("pe", lambda e, b=b, bs=bs, s=s: [
                        e.matmul(psb[b][0:32, 0:256], lhsT=h[:, k, bs * 32:(bs + 1) * 32],
                                 rhs=wslot[:, s, k * 256:(k + 1) * 256], start=(k == 0), stop=(k == 7))
                        for k in range(8)], reads=hk + [("wslot", s)], writes=(("ps", b),))
                    evac_copy(Vn[0:32, bs, i * 256:(i + 1) * 256], psb[b][0:32, 0:256], reads=(("ps", b),) + AR,
                              writes=(("Vn", bs, i),))

        def v_store():
            xk = [("xin", blk) for blk in range(nb)]
            if sample:
                P.add("pool", lambda e: e.dma_start(out=vs_d.ap(), in_=xin[:, 0, :]), reads=xk, writes=("vout",),
                      slot="st_v")
            else:
                dst = bass.AP(vp_d, tl["s"] * 512 * D, [[D, 128], [128 * D, 4], [1, D]])
                P.add("pool", lambda e, dst=dst: e.dma_start(out=dst, in_=xin[:]), reads=xk, writes=("vout",), slot="st_v")

        def qk_proj(i):
            sq_, sk_ = wload("wqkv", i), wload("wqkv", 4 + i)
            if i == 0:
                pro = proj_h_multi([(sq_, 0), (sq_, 1), (sk_, 0), (sk_, 1)], TW)
            for half in range(2):
                c = 2 * i + half
                b = pro[half] if i == 0 else proj_fm("wqkv", i, half, sq_, lambda k: h[:, k, 0:TW], hk, TW)
                q_evac(c, b)
            for half in range(2):
                c = 2 * i + half
                b = pro[2 + half] if i == 0 else proj_fm("wqkv", 4 + i, half, sk_, lambda k: h[:, k, 0:TW], hk, TW)
                k_evac(c, b)
            if want_cache:
                k_tokmajor(i, sk_)

        if sample:
            for i in range(4):
                qk_proj(i)
            k_store()
            for i in range(4):
                v_proj(i)
            v_store()

        def finish_pair(hp, bO, bD, c0, c1):
            rt = rec_t[hp % 2]
            P.add("dve", lambda e: e.reciprocal(out=rt[:, c0:c1], in_=psb[bD][:, c0:c1]), reads=(("ps", bD),) + AR,
                  writes=(("rec", hp % 2),))
            P.add("dve", lambda e: e.tensor_tensor(out=sqv[:, hp, c0:c1], in0=psb[bO][:, c0:c1], in1=rt[:, c0:c1],
                                                   op=ALU.mult),
                  reads=(("ps", bO), ("rec", hp % 2)) + AR, writes=(("sqv", hp),))

        def finish_head(hp, e_, bH):
            pb = 64 * e_
            po = 64 - pb
            rt = rec_t[e_]
            P.add("dve", lambda e: e.reciprocal(out=rt[pb:pb + 64, 0:TW], in_=psb[bH][po:po + 64, 0:TW]),
                  reads=(("ps", bH),) + AR, writes=(("rec", e_),))
            P.add("dve", lambda e: e.tensor_tensor(out=sqv[pb:pb + 64, hp, 0:TW], in0=psb[bH][pb:pb + 64, 0:TW],
                                                   in1=rt[pb:pb + 64, 0:TW], op=ALU.mult),
                  reads=(("ps", bH), ("rec", e_)) + AR, writes=(("sqv", hp),))

        if not sample:
            first = tl["first"]
            kbs = [4, 5, 6, 7] if first else [4, 0, 1, 2, 3, 5, 6, 7]
            hprev = 1 - half_cur
            LA = 3
            pending = []

            def flush(keep):
                while len(pending) > keep:
                    pending.pop(0)()
            def attend(hp):
                for e_ in range(2):
                    hd = 2 * hp + e_
                    bH = P.bank("att_acc", AB)
                    for ikb, kb in enumerate(kbs):
                        w0, w1 = 2 * kb, 2 * kb + 1
                        lo, hi = max(0, w0 - 8), min(7, w1)
                        nq = (hi - lo + 1) * 64
                        if kb < 4:
                            koff = hprev * 512 + kb * 128
                            kkeys = [("kT", hp, hprev)]
                            kbp = hprev * 4 + kb
                            corner = (0, (hi - lo) * 64)
                        else:
                            koff = half_cur * 512 + (kb - 4) * 128
                            kkeys = [("kT", hp, half_cur)]
                            kbp = half_cur * 4 + (kb - 4)
                            corner = (1, 0)
                        mlo = (lo + 8 - 2 * kb) * 64
                        nw = 0 if mlo >= 255 else min(nq, 256 - mlo)
                        bS = P.bank("att_s", SB)
                        pt_i = P.bank("att_pt", (0, 1, 2, 3))
                        pt = PT[pt_i]

                        def sfn_(e, bS=bS, hp=hp, e_=e_, koff=koff, lo=lo, hi=hi, nq=nq, hd=hd, mlo=mlo, nw=nw):
                            r = [e.matmul(psb[bS][:, 0:nq], lhsT=kT[:, hp, koff:koff + 128],
                                          rhs=qTz[:, hp, e_, lo * 64:(hi + 1) * 64], start=True, stop=(nw == 0),
                                          skip_group_check=True)]
                            if nw > 0:
                                r.append(e.matmul(psb[bS][:, 0:nw], lhsT=ident_bf[:], rhs=BT[:, hd, mlo:mlo + nw],
                                                  start=False, stop=True, skip_group_check=True))
                            return r
                        P.add("pe", sfn_, reads=kkeys + [("qT", hp), "qz", "ident_bf"] + list(BTK) + list(AR),
                              writes=(("ps", bS),))
                        P.add("act", lambda e, bS=bS, pt=pt, nq=nq: e.activation(out=pt[:, 0:nq], in_=psb[bS][:, 0:nq],
                                                                                 func=AF.Exp),
                              reads=(("ps", bS),) + AR, writes=(("PT", pt_i),))
                        cx, cc = corner
                        P.add("pool", lambda e, pt=pt, cx=cx, cc=cc: e.memset(pt[64 * cx:64 * cx + 64, cc:cc + 64], 0.0),
                              reads=AR, writes=(("PT", pt_i),))
                        vkeys = [("Vb", kbp, i) for i in range(4)]

                        def pvfn(e, bH=bH, pt=pt, kbp=kbp, hp=hp, e_=e_, lo=lo, hi=hi, nq=nq, ikb=ikb, nk=len(kbs)):
                            return [e.matmul(psb[bH][:, lo * 64:(hi + 1) * 64], lhsT=Vb[:, kbp, hp, e_ * 64:e_ * 64 + 128],
                                             rhs=pt[:, 0:nq], start=(ikb == 0), stop=(ikb == nk - 1),
                                             skip_group_check=True)]

                        def emit_pv(pvfn=pvfn, vkeys=vkeys, pt_i=pt_i, bH=bH, hp=hp, e_=e_, last=(ikb == len(kbs) - 1)):
                            P.add("pe", pvfn, reads=vkeys + [("PT", pt_i), "BTb"] + list(AR), writes=(("ps", bH),))
                            if last:
                                finish_head(hp, e_, bH)
                        pending.append(emit_pv)
                        flush(LA)
            for i in range(4):
                if i > 0:
                    mm_pool[0] = ("att_s", SB)
                qk_proj(i)
                v_proj(i)
                attend(2 * i)
                attend(2 * i + 1)
                flush(0)
            mm_pool[0] = None
            if want_cache:
                k_store()
                v_store()
        else:
            for bs in range(4):
                P.add("sp", lambda e, bs=bs: e.dma_start(
                    out=xin[:], in_=bass.AP(ck_d, bs * 512 * D, [[D, 128], [128 * D, 4], [1, D]])),
                    writes=[("xin", blk) for blk in range(4)], slot="ld_x")
                P.add("pool", lambda e, bs=bs: e.dma_start(
                    out=Vc, in_=bass.AP(cv_d, bs * 512 * D, [[D, 128], [128 * D, 4], [1, D]])),
                    reads=AR, writes=("Vc",), slot="ld_vc")
                for j in range(8):
                    b = P.bank("att_s", SB)
                    P.add("pe", lambda e, b=b, j=j: [e.transpose(out=psb[b][:, kb * 128:(kb + 1) * 128],
                                                                 in_=xin[:, kb, j * 128:(j + 1) * 128], identity=ident[:])
                                                     for kb in range(4)],
                          reads=[("xin", blk) for blk in range(4)] + ["ident"], writes=(("ps", b),))
                    evac_copy(kTc[:, j, :], psb[b][:, 0:512], reads=(("ps", b),) + AR, writes=(("kTc", j),))
                spend = []
                for hp in range(8):
                    bO = AB[(hp % 2) * 2]
                    bD = AB[(hp % 2) * 2 + 1]
                    for e_ in range(2):
                        hd = 2 * hp + e_
                        pb = 64 * e_
                        bS = P.bank("att_s", SB)
                        pt_i = P.bank("att_pt", (0, 1, 2, 3))
                        pt = PT[pt_i]

                        def sfn(e, bS=bS, pb=pb, hp=hp, hd=hd, bs=bs):
                            r = []
                            qa = qT[pb:pb + 64, hp, bs * 32:(bs + 1) * 32]
                            for kb in range(4):
                                r.append(e.matmul(psb[bS][:, kb * 32:(kb + 1) * 32],
                                                  lhsT=kTc[pb:pb + 64, hp, kb * 128:(kb + 1) * 128], rhs=qa,
                                                  start=True, stop=(512 - 128 * kb >= 255), skip_group_check=True,
                                                  tile_position=(pb, 0)))
                                mlo = 512 - 128 * kb
                                if mlo < 255:
                                    r.append(e.matmul(psb[bS][:, kb * 32:(kb + 1) * 32], lhsT=ident_bf[:],
                                                      rhs=BT[:, hd, mlo:mlo + 32], start=False, stop=True,
                                                      skip_group_check=True, tile_position=(0, 0)))
                            r.append(e.matmul(psb[bS][0:32, 128:160], lhsT=kT[pb:pb + 64, hp, bs * 32:(bs + 1) * 32],
                                              rhs=qa, start=True, stop=False, skip_group_check=True,
                                              tile_position=(pb, 0)))
                            r.append(e.matmul(psb[bS][0:32, 128:160], lhsT=ident_bf[0:32, 0:32], rhs=BT[0:32, hd, 0:32],
                                              start=False, stop=True, skip_group_check=True, tile_position=(0, 0)))
                            return r
                        P.add("pe", sfn, reads=[("kTc", hp), ("kT", hp, 0), ("qT", hp), "ident_bf"] + list(BTK) + list(AR),
                              writes=(("ps", bS),))
                        P.add("act", lambda e, bS=bS, pt=pt: e.activation(out=pt[:, 0:128], in_=psb[bS][:, 0:128],
                                                                          func=AF.Exp),
                              reads=(("ps", bS),) + AR, writes=(("PT", pt_i, "a"),))
                        P.add("act", lambda e, bS=bS, pt=pt: e.activation(out=pt[0:32, 128:160],
                                                                          in_=psb[bS][0:32, 128:160], func=AF.Exp),
                              reads=(("ps", bS),) + AR, writes=(("PT", pt_i, "b"),))

                        def pvs(e, bO=bO, bD=bD, pb=pb, pt=pt, hd=hd, bs=bs):
                            r = []
                            oo = psb[bO][pb:pb + 64, bs * 32:(bs + 1) * 32]
                            dd = psb[bD][pb:pb + 64, bs * 32:(bs + 1) * 32]
                            for kb in range(4):
                                r.append(e.matmul(oo, lhsT=Vc[:, kb, hd * 64:(hd + 1) * 64], rhs=pt[:, kb * 32:(kb + 1) * 32],
                                                  start=(kb == 0), stop=False, skip_group_check=True,
                                                  tile_position=(0, pb)))
                                r.append(e.matmul(dd, lhsT=ones_bf[:, 0:64], rhs=pt[:, kb * 32:(kb + 1) * 32],
                                                  start=(kb == 0), stop=False, skip_group_check=True,
                                                  tile_position=(0, pb)))
                            r.append(e.matmul(oo, lhsT=Vn[0:32, bs, hd * 64:(hd + 1) * 64], rhs=pt[0:32, 128:160],
                                              start=False, stop=True, skip_group_check=True, tile_position=(0, pb)))
                            r.append(e.matmul(dd, lhsT=ones_bf[0:32, 0:64], rhs=pt[0:32, 128:160],
                                              start=False, stop=True, skip_group_check=True, tile_position=(0, pb)))
                            return r
                        def emit_pvs(pvs=pvs, pt_i=pt_i, bs=bs, bO=bO, bD=bD, hp=hp, last=(e_ == 1)):
                            P.add("pe", pvs, reads=["Vc", ("PT", pt_i, "a"), ("PT", pt_i, "b"), "ones"]
                                  + [("Vn", bs, i) for i in range(4)] + list(AR),
                                  writes=(("ps", bO), ("ps", bD)))
                            if last:
                                finish_pair(hp, bO, bD, bs * 32, (bs + 1) * 32)
                        spend.append(emit_pvs)
                        while len(spend) > 2:
                            spend.pop(0)()
                while spend:
                    spend.pop(0)()

    tiles = []
    for s_ in range(4):
        for t_ in range(4):
            tiles.append(dict(s=s_, t=t_, TW=512, sample=False, first=(t_ == 0), last=(t_ == 3)))
    tiles = tiles[:n_prompt_tiles]
    if do_sample:
        tiles.append(dict(s=None, t=0, TW=128, sample=True, first=True, last=True))

    def load_x(tl):
        if tl["sample"]:
            P.add("sp", lambda e: e.dma_start(out=xin[:, 0, :], in_=xs_d.ap()), writes=(("xin", 0),), slot="ld_x")
        else:
            r0 = tl["s"] * SEQ + tl["t"] * 512
            src = bass.AP(xp_d, r0 * D, [[D, 128], [128 * D, 4], [1, D]])
            P.add("sp", lambda e: e.dma_start(out=xin[:], in_=src), writes=[("xin", blk) for blk in range(4)],
                  slot="ld_x")

    load_x(tiles[0])
    for ti, tl in enumerate(tiles):
        TW = tl["TW"]
        nb = TW // 128
        for j in range(8):
            b = P.bank()
            P.add("pe", lambda e, b=b, j=j, nb=nb: [e.transpose(out=psb[b][:, blk * 128:(blk + 1) * 128],
                                                                in_=xin[:, blk, j * 128:(j + 1) * 128], identity=ident[:])
                                                    for blk in range(nb)],
                  reads=[("xin", blk) for blk in range(nb)] + ["ident"], writes=(("ps", b),))
            evac_copy(x[:, j, 0:TW], psb[b][:, 0:TW], reads=(("ps", b),), writes=(("x", j),))
        nxt = tiles[ti + 1] if ti + 1 < len(tiles) else None
        prefetch_early = nxt is not None and not (do_l1 and (tl["last"] or tl["sample"]))
        checkpoint("xT")
        prenorm(0, TW)
        checkpoint("prenorm0")
        conv_mixer(tl)
        if prefetch_early:
            load_x(nxt)
        checkpoint("conv")
        postnorm(1, TW, out_proj("wout0", TW))
        checkpoint("post0")
        ffn(0, TW)
        checkpoint("ffn0")
        if do_l1:
            prenorm(4, TW)
            attention(tl)
            postnorm(5, TW, out_proj("wo", TW, kouter_banks=None if tl["sample"] else (4, 5, 6, 7)))
            if nxt is not None and not prefetch_early:
                load_x(nxt)
            ffn(1, TW)
        for blk in range(nb):
            for jg in range(2):
                b = P.bank()
                P.add("pe", lambda e, b=b, blk=blk, jg=jg: [
                    e.transpose(out=psb[b][:, jj * 128:(jj + 1) * 128], in_=x[:, jg * 4 + jj, blk * 128:(blk + 1) * 128],
                                identity=ident[:]) for jj in range(4)],
                    reads=[("x", jg * 4 + jj) for jj in range(4)] + ["ident"], writes=(("ps", b),))
                mst = m[:].rearrange("p j t -> p (j t)")[:, (blk * 2 + jg) * 512:(blk * 2 + jg + 1) * 512]
                evac_copy(mst, psb[b][:, 0:512], reads=(("ps", b),), writes=(("m", blk * 2 + jg),))
        mk = [("m", n) for n in range(2 * nb)]
        if tl["sample"]:
            P.add("pool", lambda e: e.dma_start(out=ys_d.ap(), in_=m[:].rearrange("p j t -> p (j t)")[:, 0:1024]),
                  reads=mk, writes=("yout",), slot="st_y")
        else:
            r0 = tl["s"] * SEQ + tl["t"] * 512
            dst = bass.AP(yp_d, r0 * D, [[D, 128], [128 * D, 4], [1, D]])
            P.add("pool", lambda e, dst=dst: e.dma_start(out=dst, in_=m[:].rearrange("p (b j) t -> p b (j t)", b=4)),
                  reads=mk, writes=("yout",), slot="st_y")
    return


def build_nc(**kw):
    nc = bass.Bass("TRN2", target_bir_lowering=False)
    with contextlib.ExitStack() as stack:
        P = build_program(nc, stack, **kw)
        per = P.finalize(nc, stack)
        block = stack.enter_context(nc.Block())
        names = {"pe": "tensor", "act": "scalar", "dve": "vector", "pool": "gpsimd", "sp": "sync"}
        for en in ("sp", "pool", "pe", "act", "dve"):
            ops = per.get(en, [])

            def body(eng, en=en, ops=ops):
                P.emit_engine(en, eng, ops, final_wait=(en == "sp"))
            getattr(block, names[en])(body)
    return nc


def make_in_maps(inputs):
    f = lambda a: np.ascontiguousarray(np.asarray(a, dtype=np.float32))
    norms = np.stack([inputs["norm_mix_pre"][0], inputs["norm_mix_post"][0], inputs["norm_ffn_pre"][0],
                      inputs["norm_ffn_post"][0], inputs["norm_mix_pre"][1], inputs["norm_mix_post"][1],
                      inputs["norm_ffn_pre"][1], inputs["norm_ffn_post"][1]], axis=0)
    shared = {
        "win0": f(inputs["conv_w_in"][0]), "wout0": f(inputs["conv_w_out"][0]),
        "wqkv": f(inputs["attn_w_qkv"][0]), "wo": f(inputs["attn_w_o"][0]),
        "wgu0": f(inputs["ffn_w_gate_up"][0]), "wgu1": f(inputs["ffn_w_gate_up"][1]),
        "wdn0": f(inputs["ffn_w_down"][0]), "wdn1": f(inputs["ffn_w_down"][1]),
        "convk": f(inputs["conv_kernel"][0]), "relb": f(inputs["attn_rel_bias"][0]),
        "norms": f(norms), "ident": np.eye(128, dtype=np.float32),
    }
    maps = []
    for c in range(NCORES):
        sl = slice(4 * c, 4 * c + 4)
        d = dict(shared)
        d["xp"] = f(inputs["x_prompt"][sl]).reshape(4 * SEQ, D)
        d["xs"] = f(inputs["x_sample"][sl]).reshape(128, D)
        d["sconv"] = f(inputs["state_conv"][0, sl]).reshape(8, D)
        d["ck"] = f(inputs["cache_k"][0, sl]).reshape(4 * 512, D)
        d["cv"] = f(inputs["cache_v"][0, sl]).reshape(4 * 512, D)
        maps.append(d)
    return maps


def assemble(results):
    cat = lambda k: np.concatenate([np.asarray(r[k], dtype=np.float32) for r in results], axis=0)
    yp = cat("yp").reshape(32, SEQ, D)
    ys = cat("ys").reshape(32, 32, D)
    ncp = cat("ncp").reshape(1, 32, 2, D)
    ncs = cat("ncs").reshape(1, 32, 2, D)
    kp = cat("kp").reshape(1, 32, 512, 16, 64)
    vp = cat("vp").reshape(1, 32, 512, 16, 64)
    ks = cat("ks").reshape(1, 32, 32, 16, 64)
    vs = cat("vs").reshape(1, 32, 32, 16, 64)
    return (yp, ys, ncp, ncs, kp, vp, ks, vs)


def kernel(**inputs):
    inputs = {k: np.asarray(v) for k, v in inputs.items()}
    nc = build_nc()
    res = run_bass_kernel_spmd(nc, make_in_maps(inputs), core_ids=list(range(NCORES)))
    return assemble(res.results)
```
